# Optimizing a Trainium2 kernel written in Bass

```python
import jax, jax.numpy as jnp
from jax import lax
import numpy as np

D_MODEL = 1024
BATCH = 1
SEQ = 16384
DEPTH = 4
DEC_BATCH = 8
DEC_SEQ = 8192
PAST_LEN = 128

GLA_HEADS = 4
GLA_DK = D_MODEL // 2 // GLA_HEADS
GLA_DV = D_MODEL // GLA_HEADS
GLA_LOWRANK = 16
GLA_TAU = 16.0
GLA_CHUNK = 64
QK_W = GLA_HEADS * GLA_DK
V_W = GLA_HEADS * GLA_DV
POOL_GROUPS = 4
POOL_WIDTH = D_MODEL // 2
POOL_GC = POOL_WIDTH // POOL_GROUPS
POOL_WINDOWS = (2, 4, 8, 16)
D_FF = 2816
CONV_W = 3
EPS = 1e-6

IN_WIDTHS = (QK_W, QK_W, V_W, V_W, GLA_LOWRANK, GLA_LOWRANK, POOL_WIDTH, D_MODEL, D_MODEL)
IN_WIDTH = int(sum(IN_WIDTHS))
IN_SPLITS = tuple(int(c) for c in np.cumsum(IN_WIDTHS)[:-1])

kernel_name = "hybrid_gla_pool_convffn_encoder"


def rmsnorm(x, g):
    x32 = x.astype(jnp.float32)
    y = x32 * lax.rsqrt(jnp.mean(x32 * x32, axis=-1, keepdims=True) + EPS) * g.astype(jnp.float32)
    return y.astype(x.dtype)


def gla_causal(q, k, v, g):
    B, H, S, DK = q.shape
    DV = v.shape[-1]
    n = S // GLA_CHUNK

    def to_chunks(t):
        return jnp.moveaxis(t.reshape(B, H, n, GLA_CHUNK, t.shape[-1]), 2, 0)

    qc, kc, vc, gc = to_chunks(q), to_chunks(k), to_chunks(v), to_chunks(g)
    mask = jnp.tril(jnp.ones((GLA_CHUNK, GLA_CHUNK), dtype=bool))[:, :, None]

    def step(state, inp):
        qi, ki, vi, gi = inp
        b = jnp.cumsum(gi, axis=2)
        o_inter = jnp.einsum('bhik,bhkv->bhiv', qi * jnp.exp(b), state)
        diff = b[:, :, :, None, :] - b[:, :, None, :, :]
        decay = jnp.exp(jnp.where(mask, diff, -jnp.inf))
        scores = jnp.einsum('bhik,bhijk,bhjk->bhij', qi, decay, ki)
        o_intra = jnp.einsum('bhij,bhjv->bhiv', scores, vi)
        b_last = b[:, :, -1:, :]
        k_dec = ki * jnp.exp(b_last - b)
        state = jnp.exp(b_last[:, :, 0, :])[..., None] * state + jnp.einsum('bhjk,bhjv->bhkv', k_dec, vi)
        return state, o_inter + o_intra

    s0 = jnp.zeros((B, H, DK, DV), jnp.float32)
    _, o = lax.scan(step, s0, (qc, kc, vc, gc))
    return jnp.moveaxis(o, 0, 2).reshape(B, H, S, DV)


def pool_branch(u, w_grp, scale):
    B, S, _ = u.shape
    ug = u.reshape(B, S, POOL_GROUPS, POOL_GC).astype(jnp.float32)
    cs = jnp.concatenate([jnp.zeros((B, 1, POOL_GROUPS, POOL_GC), jnp.float32), jnp.cumsum(ug, axis=1)], axis=1)
    pos = jnp.arange(S)
    outs = []
    for gi, w in enumerate(POOL_WINDOWS):
        lo = jnp.clip(pos - w // 2, 0, S - 1)
        hi = jnp.clip(pos + (w - 1 - w // 2), 0, S - 1)
        csg = cs[:, :, gi]
        wsum = jnp.take(csg, hi + 1, axis=1) - jnp.take(csg, lo, axis=1)
        cnt = (hi - lo + 1).astype(jnp.float32)[None, :, None]
        outs.append(wsum / cnt - ug[:, :, gi])
    p = jnp.stack(outs, axis=2)
    p = jnp.einsum('bsgc,gcd->bsgd', p, w_grp.astype(jnp.float32)).reshape(B, S, POOL_WIDTH)
    return (p * scale.astype(jnp.float32)).astype(u.dtype)


def mixer(xn, w_in, w_lr2_f, b_lr_f, w_lr2_b, b_lr_b, onorm_g, w_pool_grp, pool_scale, w_br_a, w_br_b, w_out):
    B, S, _ = xn.shape
    h = xn @ w_in
    q, k, v, r, lr_f, lr_b, u, ga, gb = jnp.split(h, IN_SPLITS, axis=-1)

    def heads(t):
        return t.reshape(B, S, GLA_HEADS, -1).transpose(0, 2, 1, 3).astype(jnp.float32)

    qh = heads(q) * (GLA_DK ** -0.5)
    kh = heads(k)
    vh = heads(v)
    g_f = heads(jax.nn.log_sigmoid((lr_f @ w_lr2_f + b_lr_f).astype(jnp.float32)) / GLA_TAU)
    g_b = heads(jax.nn.log_sigmoid((lr_b @ w_lr2_b + b_lr_b).astype(jnp.float32)) / GLA_TAU)

    def flip(t):
        return jnp.flip(t, axis=2)

    o = gla_causal(qh, kh, vh, g_f) + flip(gla_causal(flip(qh), flip(kh), flip(vh), flip(g_b)))
    o = o * lax.rsqrt(jnp.mean(o * o, axis=-1, keepdims=True) + EPS)
    o = o.transpose(0, 2, 1, 3).reshape(B, S, V_W) * onorm_g.astype(jnp.float32)
    a_out = (o.astype(xn.dtype) * jax.nn.silu(r)) @ w_br_a

    b_out = pool_branch(u, w_pool_grp, pool_scale) @ w_br_b

    merged = jax.nn.sigmoid(ga) * a_out + jax.nn.sigmoid(gb) * b_out
    return merged @ w_out


def conv_ffn(xn, w_up, conv_w, conv_b, w_down):
    h = xn @ w_up
    a, val = jnp.split(h, [D_FF], axis=-1)
    ap = jnp.pad(a, ((0, 0), (1, 1), (0, 0)))
    a = ap[:, :-2] * conv_w[0] + ap[:, 1:-1] * conv_w[1] + ap[:, 2:] * conv_w[2] + conv_b
    return (jax.nn.silu(a) * val) @ w_down


def trunk(x, norm1_g, w_in, w_lr2_f, b_lr_f, w_lr2_b, b_lr_b, onorm_g, w_pool_grp, pool_scale,
          w_br_a, w_br_b, w_out, norm2_g, w_up, conv_w, conv_b, w_down, final_g):
    for l in range(DEPTH):
        xn = rmsnorm(x, norm1_g[l])
        x = x + mixer(xn, w_in[l], w_lr2_f[l], b_lr_f[l], w_lr2_b[l], b_lr_b[l], onorm_g[l],
                      w_pool_grp[l], pool_scale[l], w_br_a[l], w_br_b[l], w_out[l])
        xn = rmsnorm(x, norm2_g[l])
        x = x + conv_ffn(xn, w_up[l], conv_w[l], conv_b[l], w_down[l])
    return rmsnorm(x, final_g)


def setup_inputs(seed: int = 0) -> dict:
    key = jax.random.key(seed)
    ks = jax.random.split(key, 24)

    def nrm(k, shape, scale):
        return jax.random.normal(k, shape, jnp.float32) * scale

    L = DEPTH
    return {
        "x_prompt": nrm(ks[0], (BATCH, SEQ, D_MODEL), 1.0),
        "x_sample": nrm(ks[1], (DEC_BATCH, DEC_SEQ, D_MODEL), 1.0),
        "norm1_g": 1.0 + nrm(ks[2], (L, D_MODEL), 0.02),
        "w_in": nrm(ks[3], (L, D_MODEL, IN_WIDTH), D_MODEL ** -0.5),
        "w_lr2_f": nrm(ks[4], (L, GLA_LOWRANK, QK_W), GLA_LOWRANK ** -0.5),
        "b_lr_f": nrm(ks[5], (L, QK_W), 0.1),
        "w_lr2_b": nrm(ks[6], (L, GLA_LOWRANK, QK_W), GLA_LOWRANK ** -0.5),
        "b_lr_b": nrm(ks[7], (L, QK_W), 0.1),
        "onorm_g": 1.0 + nrm(ks[8], (L, V_W), 0.02),
        "w_pool_grp": nrm(ks[9], (L, POOL_GROUPS, POOL_GC, POOL_GC), POOL_GC ** -0.5),
        "pool_scale": 1.0 + nrm(ks[10], (L, POOL_WIDTH), 0.02),
        "w_br_a": nrm(ks[11], (L, V_W, D_MODEL), V_W ** -0.5),
        "w_br_b": nrm(ks[12], (L, POOL_WIDTH, D_MODEL), POOL_WIDTH ** -0.5),
        "w_out": nrm(ks[13], (L, D_MODEL, D_MODEL), D_MODEL ** -0.5),
        "norm2_g": 1.0 + nrm(ks[14], (L, D_MODEL), 0.02),
        "w_up": nrm(ks[15], (L, D_MODEL, 2 * D_FF), D_MODEL ** -0.5),
        "conv_w": nrm(ks[16], (L, CONV_W, D_FF), CONV_W ** -0.5),
        "conv_b": nrm(ks[17], (L, D_FF), 0.02),
        "w_down": nrm(ks[18], (L, D_FF, D_MODEL), D_FF ** -0.5),
        "final_g": 1.0 + nrm(ks[19], (D_MODEL,), 0.02),
    }


def reference(x_prompt, x_sample, norm1_g, w_in, w_lr2_f, b_lr_f, w_lr2_b, b_lr_b, onorm_g, w_pool_grp,
              pool_scale, w_br_a, w_br_b, w_out, norm2_g, w_up, conv_w, conv_b, w_down, final_g):
    y_prompt = trunk(x_prompt, norm1_g, w_in, w_lr2_f, b_lr_f, w_lr2_b, b_lr_b, onorm_g, w_pool_grp, pool_scale,
                     w_br_a, w_br_b, w_out, norm2_g, w_up, conv_w, conv_b, w_down, final_g)
    y_sample = trunk(x_sample, norm1_g, w_in, w_lr2_f, b_lr_f, w_lr2_b, b_lr_b, onorm_g, w_pool_grp, pool_scale,
                     w_br_a, w_br_b, w_out, norm2_g, w_up, conv_w, conv_b, w_down, final_g)
    return (y_prompt, y_sample)
```

```python
import numpy as np
from contextlib import ExitStack
import concourse.bass as bass
import concourse.mybir as mybir
from concourse.bass_utils import run_bass_kernel_spmd
from concourse.ap import AP

F32 = mybir.dt.float32
BF16 = mybir.dt.bfloat16
ALU = mybir.AluOpType
AF = mybir.ActivationFunctionType

D = 1024
KC = 8
H = 4
DK = 128
DV = 256
INW = 5664
DFF = 2816
FC = 22
EPS = 1e-6
Q0, K0, V0, R0, LR0, U0, GA0, GB0 = 0, 512, 1024, 2048, 3072, 3104, 3616, 4640
POOL_WINDOWS = (2, 4, 8, 16)

CFG = dict(UNIT=8192, DEPTH=4, NCORES=8)


class Buf:
    __slots__ = ("name", "last_w", "readers")

    def __init__(self, name):
        self.name = name
        self.last_w = None
        self.readers = []


class Op:
    __slots__ = ("eng", "fn", "reads", "writes", "dma", "key", "kidx", "waits", "signal", "value", "barrier")

    def __init__(self, eng, fn, reads, writes, dma):
        self.eng = eng
        self.fn = fn
        self.reads = reads
        self.writes = writes
        self.dma = dma
        self.key = None
        self.kidx = 0
        self.waits = []
        self.signal = False
        self.value = 0
        self.barrier = False


ENGS = ("pe", "dve", "act", "pool", "sp")


class Prog:
    def __init__(self, nc):
        self.nc = nc
        self.ops = []
        self.kcount = {}

    def add(self, eng, fn, reads=(), writes=(), dma=None):
        op = Op(eng, fn, tuple(reads), tuple(writes), dma)
        op.key = ("dma:" + dma) if dma is not None else eng
        c = self.kcount.get(op.key, 0) + 1
        self.kcount[op.key] = c
        op.kidx = c
        self.ops.append(op)
        return op

    def pe(self, fn, reads=(), writes=()):
        return self.add("pe", fn, reads, writes)

    def dve(self, fn, reads=(), writes=()):
        return self.add("dve", fn, reads, writes)

    def act(self, fn, reads=(), writes=()):
        return self.add("act", fn, reads, writes)

    def pool(self, fn, reads=(), writes=()):
        return self.add("pool", fn, reads, writes)

    def dma(self, group, out, in_, reads=(), writes=(), slow=False):
        if slow:
            return self.add("sp", lambda e: e.dma_start(out=out, in_=in_, allow_slow_non_contiguous=True), reads, writes, dma=group)
        return self.add("sp", lambda e: e.dma_start(out=out, in_=in_), reads, writes, dma=group)

    def barrier(self):
        op = Op("sp", None, (), (), None)
        op.barrier = True
        self.ops.append(op)

    def analyze(self):
        known = {e: {} for e in ENGS}
        last_by_key = {}
        pending = {e: None for e in ENGS}
        for op in self.ops:
            if op.barrier:
                snap = dict(last_by_key)
                for e in ENGS:
                    pending[e] = snap
                continue
            deps = {}

            def need(p, kind):
                if p is None or p is op:
                    return
                if p.dma is None and op.dma is None and p.eng == op.eng:
                    if op.eng == "pe" or op.eng == "sp":
                        return
                    if kind != "raw":
                        return
                cur = deps.get(p.key)
                if cur is None or p.kidx > cur.kidx:
                    deps[p.key] = p

            for b in op.reads:
                need(b.last_w, "raw")
            for b in op.writes:
                need(b.last_w, "waw")
                for r in b.readers:
                    need(r, "war")
            if pending[op.eng] is not None:
                for k, p in pending[op.eng].items():
                    if p.dma is None and p.eng == op.eng and op.dma is None:
                        continue
                    cur = deps.get(k)
                    if cur is None or p.kidx > cur.kidx:
                        deps[k] = p
                pending[op.eng] = None
            kn = known[op.eng]
            for k, p in deps.items():
                if kn.get(k, 0) >= p.kidx:
                    continue
                kn[k] = p.kidx
                p.signal = True
                op.waits.append(p)
            for b in op.writes:
                b.last_w = op
                b.readers = []
            for b in op.reads:
                b.readers.append(op)
            if op.fn is not None:
                last_by_key[op.key] = op
        cnt = {}
        for op in self.ops:
            if op.signal:
                c = cnt.get(op.key, 0) + 1
                cnt[op.key] = c
                op.value = c * (16 if op.dma is not None else 1)
        self.sigcount = cnt

    def emit(self, stack):
        nc = self.nc
        self.analyze()
        sems = {}
        for k in self.sigcount:
            sems[k] = stack.enter_context(nc.semaphore("s_" + k.replace(":", "_")))
        per = {e: [] for e in ENGS}
        for op in self.ops:
            if not op.barrier:
                per[op.eng].append(op)
        block = stack.enter_context(nc.Block())

        def run(eng_name):
            def body(eng):
                for op in per[eng_name]:
                    for p in op.waits:
                        eng.wait_ge(sems[p.key], p.value)
                    if op.fn is not None:
                        ins = op.fn(eng)
                        if op.signal:
                            ins.then_inc(sems[op.key], 16 if op.dma is not None else 1)
            return body

        block.tensor(run("pe"))
        block.vector(run("dve"))
        block.scalar(run("act"))
        block.gpsimd(run("pool"))
        block.sync(run("sp"))
        return {k: len(v) for k, v in per.items()}, len(sems)


class Arena:
    def __init__(self, nc, stack, nbytes):
        self.t = stack.enter_context(nc.sbuf_tensor("arena", [128, nbytes // 2], BF16))
        self.cap = nbytes
        self.off = 0

    def alloc(self, shape, dt, parts=128, at=None):
        n = int(np.prod(shape))
        esz = 2 if dt == BF16 else 4
        nb = (n * esz + 63) // 64 * 64
        if at is None:
            o = self.off
            self.off += nb
            assert self.off <= self.cap, ("SBUF arena overflow", self.off, self.cap)
        else:
            o = at
        v = self.t[0:parts, o // 2: o // 2 + n * esz // 2]
        if dt == F32:
            v = v.bitcast(F32)
        if len(shape) == 2:
            v = v.rearrange("p (a b) -> p a b", a=shape[0])
        elif len(shape) == 3:
            v = v.rearrange("p (a b c) -> p a b c", a=shape[0], b=shape[1])
        return v


def rev2(v):
    (ps, pc), (es, en) = v.ap
    return AP(v.tensor, v.offset + (en - 1) * es, [[ps, pc], [-es, en]])


def small_layout(inp, L):
    def pp(a, nch):
        return np.ascontiguousarray(a.reshape(L, nch, 128).transpose(2, 0, 1).reshape(128, L * nch))
    parts = [pp(inp["norm1_g"], 8), pp(inp["norm2_g"], 8), pp(inp["onorm_g"], 8), pp(inp["pool_scale"], 4),
             pp(inp["b_lr_f"], 4), pp(inp["b_lr_b"], 4),
             np.ascontiguousarray(inp["conv_w"].reshape(L, 3, FC, 128).transpose(3, 0, 1, 2).reshape(128, L * 3 * FC)),
             pp(inp["conv_b"], FC),
             np.ascontiguousarray(inp["final_g"].reshape(8, 128).T)]
    return np.concatenate(parts, axis=1).astype(np.float32)


def sv_offsets(L):
    o = {}
    c = 0
    for name, n in [("g1", 8 * L), ("g2", 8 * L), ("og", 8 * L), ("ps", 4 * L), ("bf", 4 * L), ("bb", 4 * L),
                    ("cw", 3 * FC * L), ("cb", FC * L), ("fg", 8)]:
        o[name] = c
        c += n
    o["_n"] = c
    return o


def band_matrices(cls):
    out = np.zeros((4, 144, 128), np.float32)
    for gi, w in enumerate(POOL_WINDOWS):
        for t in range(128):
            lo = t - w // 2
            hi = t + (w - 1 - w // 2)
            if cls == "first":
                lo = max(lo, 0)
            if cls == "last":
                hi = min(hi, 127)
            cnt = hi - lo + 1
            for tp in range(lo, hi + 1):
                if tp < 0:
                    row = 128 + (tp + 8)
                elif tp > 127:
                    row = 136 + (tp - 128)
                else:
                    row = tp
                out[gi, row, t] += 1.0 / cnt
            out[gi, t, t] -= 1.0
    return out


def build_program(cfg):
    UNIT = cfg["UNIT"]
    L = cfg["DEPTH"]
    NT = 2 * UNIT
    NTF = NT // 512
    NCK = NT // 128
    TC = 256
    NTC = NT // TC
    SVO = sv_offsets(L)
    NS = SVO["_n"]

    nc = bass.Bass("TRN2", target_bir_lowering=False)

    def din(name, shape):
        return nc.dram_tensor(name, shape, F32, kind="ExternalInput").ap()

    xT = din("xT", [D, NT])
    svec = din("svec", [128, NS])
    cst = din("cst", [128, 4 * 128])
    bands = din("bands", [5 * 4 * 144, 128])
    flag_in = din("flag", [128, 1])
    w_in = din("w_in", [L * D, INW])
    wlr = din("wlr", [L * 32, 2 * 512])
    w_grp = din("w_grp", [L * 4 * 128, 128])
    w_ba = din("w_ba", [L * D, D])
    w_bb = din("w_bb", [L * 512, D])
    w_out = din("w_out", [L * D, D])
    w_up = din("w_up", [L * D, 2 * DFF])
    w_dn = din("w_dn", [L * DFF, D])
    Y = nc.dram_tensor("Y", [D, NT], F32, kind="ExternalOutput").ap()

    def dscr(name, shape, dt):
        return nc.dram_tensor(name, shape, dt).ap()

    Xs = dscr("Xs", [D, NT], F32)
    XM = dscr("XM", [D, NT + 2], F32)
    QB = dscr("QB", [512, NT], BF16)
    KB = dscr("KB", [NT, 512], BF16)
    Vs = dscr("Vs", [NT, 1024], BF16)
    Os = dscr("Os", [D, NT], BF16)
    Rs = dscr("Rs", [D, NT], BF16)
    GAs = dscr("GAs", [D, NT], BF16)
    GBs = dscr("GBs", [D, NT], BF16)
    Us = dscr("Us", [NT + 16, 512], BF16)

    P = Prog(nc)
    dbufs = {}

    def DB(name, t):
        k = (name, t)
        if k not in dbufs:
            dbufs[k] = Buf("%s%d" % (name, t))
        return dbufs[k]

    st = ExitStack()
    with st:
        ar = Arena(nc, st, 207 * 1024)
        pbank = [st.enter_context(nc.psum_tensor("pb%d" % i, [128, 512], F32)) for i in range(8)]
        pbuf = [Buf("pb%d" % i) for i in range(8)]

        cbf = ar.alloc([4, 128], BF16)
        ident = cbf[:, 0, :]
        ones = cbf[:, 1, :]
        mask_f = cbf[:, 2, :]
        mask_b = cbf[:, 3, :]
        zer = ar.alloc([128], F32)
        sv = ar.alloc([NS], F32)
        negb = ar.alloc([8 * L], F32)
        flag = ar.alloc([1], F32)
        c_start = ar.off
        S32 = ar.alloc([H, DV], F32)
        Sbf = ar.alloc([H, DV], BF16)
        abfirst = ar.alloc([H, NCK], F32)
        akeep = ar.alloc([H], F32)
        B_akeep = Buf('akeep')
        B_const = Buf("const")
        B_S32 = [Buf("S32_%d" % h) for h in range(H)]
        B_Sbf = [Buf("Sbf_%d" % h) for h in range(H)]
        B_abf = Buf("abfirst")
        persist_end = ar.off

        stg0 = ar.alloc([4 * 128], F32)
        P.dma("c0", stg0, cst, writes=[B_const])
        P.dve(lambda e: e.tensor_copy(cbf.rearrange("p a b -> p (a b)"), stg0), [B_const], [B_const])
        P.dma("c1", sv, svec, writes=[B_const])
        P.dma("c2", flag, flag_in, writes=[B_const])
        P.pool(lambda e: e.memset(zer, 0.0), [], [B_const])
        P.dve(lambda e: e.tensor_scalar(negb, sv[:, SVO["bf"]:SVO["bf"] + 8 * L], -1.0, None, ALU.mult), [B_const], [B_const])
        zb = ar.alloc([8, 512], BF16)
        P.pool(lambda e: e.memset(zb, 0.0), [], [B_const])
        zb3 = zb.bitcast(F32) if False else None
        zf = ar.alloc([8, 1], F32)
        P.pool(lambda e: e.memset(zf, 0.0), [], [B_const])
        P.dma("g0", Us[0:8, :], zb[0:8, 0, :], reads=[B_const], writes=[DB("U", -1)])
        P.dma("g1", Us[NT + 8:NT + 16, :], zb[0:8, 0, :], reads=[B_const], writes=[DB("U", NTF)])
        P.dma("g2", XM[:, 0:1].rearrange("(k p) n -> p k n", p=128), zf, reads=[B_const], writes=[DB("XM", -1)], slow=True)
        P.dma("g3", XM[:, NT + 1:NT + 2].rearrange("(k p) n -> p k n", p=128), zf, reads=[B_const], writes=[DB("XM", NTF)], slow=True)
        P.barrier()
        ar.off = persist_end

        rr = [0]

        def load_w(dst, src, nk, ncols, scale_col=None, cuts=(), muls=None, stg=None, stgB=None):
            edges = sorted(set([0, ncols] + [c for c in cuts if 0 < c < ncols]))
            blocks = []
            for a, b in zip(edges[:-1], edges[1:]):
                c = a
                while c < b:
                    blocks.append((c, min(b, c + 2048)))
                    c += 2048
            for k in range(nk):
                for (c0, c1) in blocks:
                    i = rr[0] % 2
                    rr[0] += 1
                    s = stg[i][:, 0:c1 - c0]
                    P.dma("stg%d" % i, s, src[k * 128:(k + 1) * 128, c0:c1], writes=[stgB[i]])
                    o = dst[:, k, c0:c1]
                    mul = None
                    if muls:
                        for (m0, m1, mv) in muls:
                            if c0 >= m0 and c1 <= m1:
                                mul = mv
                    sc = sv[:, scale_col + k:scale_col + k + 1] if scale_col is not None else None
                    eng = ("act", "dve")[rr[0] % 2]
                    if mul is not None and eng == "act":
                        eng = "dve"
                    if eng == "act":
                        if sc is None:
                            P.act(lambda e, o=o, s=s: e.copy(o, s), [stgB[i]], [])
                        else:
                            P.act(lambda e, o=o, s=s, sc=sc: e.activation(o, s, AF.Identity, scale=sc), [stgB[i], B_const], [])
                    else:
                        if sc is None and mul is None:
                            f = lambda e, o=o, s=s: e.tensor_copy(o, s)
                        elif mul is None:
                            f = lambda e, o=o, s=s, sc=sc: e.tensor_scalar(o, s, sc, None, ALU.mult)
                        elif sc is None:
                            f = lambda e, o=o, s=s, mul=mul: e.tensor_scalar(o, s, float(mul), None, ALU.mult)
                        else:
                            f = lambda e, o=o, s=s, sc=sc, mul=mul: e.tensor_scalar(o, s, sc, float(mul), ALU.mult, ALU.mult)
                        P.add(eng, f, [stgB[i], B_const], [])

        def mm(out, lhsT, rhs, start, stop, reads, writes):
            P.pe(lambda e: e.matmul(out, lhsT, rhs, start=start, stop=stop), reads, writes)

        for l in range(L):
            last_layer = (l == L - 1)
            Xsrc = xT if l == 0 else Xs
            ar.off = persist_end
            WIN = ar.alloc([KC, INW], BF16)
            WLR = ar.alloc([2, 512], BF16, parts=32)
            xt = ar.alloc([KC, 512], F32)
            xn = ar.alloc([KC, 512], BF16)
            lnt = ar.alloc([512], F32)
            rstd = ar.alloc([512], F32)
            ring_off = ar.off
            ring = [ar.alloc([4, 512], BF16) for _ in range(4)]
            stg = [ar.alloc([2048], F32, at=ring_off), ar.alloc([2048], F32, at=ring_off + 8192)]
            sq = ar.alloc([KC, 512], BF16)
            lrT = ar.alloc([512], BF16, parts=32)
            vtm = ar.alloc([4, 1024], BF16)
            utm = ar.alloc([4, 512], BF16)
            kbstg = ar.alloc([4, 512], BF16)
            ost = [ar.alloc([2, 512], BF16) for _ in range(2)]
            qbs = [ar.alloc([512], BF16) for _ in range(2)]
            el = ar.alloc([512], F32)
            bc = ar.alloc([512], F32)
            Asets = [[ar.alloc([512], F32) for _ in range(4)] for _ in range(2)]
            Anames = [["Af%d" % i_, "Aif%d" % i_, "Ab%d" % i_, "Aib%d" % i_] for i_ in range(2)]
            qf = ar.alloc([512], BF16)
            kf = ar.alloc([512], BF16)
            kbT = ar.alloc([512], BF16)
            ktm4 = ar.alloc([4, 128], BF16)
            B_P4 = Buf('P4')
            sTf = [ar.alloc([128], BF16) for _ in range(2)]
            sTb = [ar.alloc([128], BF16) for _ in range(2)]
            B = {n: Buf("F_" + n) for n in ("xt xn lnt rstd sq lrT vtm utm kbstg el bc Af0 Aif0 Ab0 Aib0 Af1 Aif1 Ab1 Aib1 qf kf kbT "
                                          "ring0 ring1 ring2 ring3 ost0 ost1 qbs0 qbs1 ktm4 sTf0 sTf1 sTb0 sTb1 stg0 stg1").split()}
            stgB = [B["stg0"], B["stg1"]]
            load_w(WIN, w_in[l * D:(l + 1) * D, :], KC, INW, scale_col=SVO["g1"] + 8 * l, cuts=(512,),
                   muls=[(0, 512, DK ** -0.5)], stg=stg, stgB=stgB)
            P.dma("stg0", stg[0][0:32, 0:1024], wlr[l * 32:(l + 1) * 32, :], writes=[stgB[0]])
            P.dve(lambda e: e.tensor_copy(WLR.rearrange("p a b -> p (a b)"), stg[0][0:32, 0:1024]), [stgB[0]], [])
            P.barrier()
            pjr = [0]
            ringi = [0]

            def next_pj():
                i = pjr[0] % 3
                pjr[0] += 1
                return i

            for t in range(NTF):
                t0 = t * 512
                unit_start = (t0 % UNIT == 0)
                if t0 == 0:
                    P.dve(lambda e: e.memset(akeep, 0.0), [], [B_akeep])
                    for h in range(H):
                        P.dve(lambda e, h=h: e.memset(S32[:, h, :], 0.0), [], [B_S32[h]])
                        P.dve(lambda e, h=h: e.memset(Sbf[:, h, :], 0.0), [], [B_Sbf[h]])
                elif unit_start:
                    for h in range(H):
                        P.dve(lambda e, h=h: e.tensor_scalar(S32[:, h, :], S32[:, h, :], flag[:, 0:1], None, ALU.mult),
                              [B_S32[h], B_const], [B_S32[h]])
                        P.dve(lambda e, h=h: e.tensor_scalar(Sbf[:, h, :], Sbf[:, h, :], flag[:, 0:1], None, ALU.mult),
                              [B_Sbf[h], B_const], [B_Sbf[h]])
                if t == 0:
                    P.dma("F_xt", xt, Xsrc[:, t0:t0 + 512].rearrange("(k p) n -> p k n", p=128),
                          reads=[DB("X", t)], writes=[B["xt"]])
                P.act(lambda e: e.activation(sq, xt, AF.Square), [B["xt"]], [B["sq"]])
                for k in range(KC):
                    mm(pbank[3][:, :], ones, sq[:, k, :], k == 0, k == KC - 1, [B["sq"], B_const], [pbuf[3]])
                P.act(lambda e: e.activation(lnt, pbank[3][:, :], AF.Ln, scale=1.0 / D, bias=EPS), [pbuf[3]], [B["lnt"]])
                P.act(lambda e: e.activation(rstd, lnt, AF.Exp, scale=-0.5), [B["lnt"]], [B["rstd"]])
                P.dve(lambda e: e.tensor_tensor(xn, xt, rstd.unsqueeze(1).broadcast_to([128, KC, 512]), ALU.mult),
                      [B["xt"], B["rstd"]], [B["xn"]])
                if t + 1 < NTF:
                    P.dma("F_xt", xt, Xsrc[:, t0 + 512:t0 + 1024].rearrange("(k p) n -> p k n", p=128),
                          reads=[DB("X", t + 1)], writes=[B["xt"]])
                pj = next_pj()
                for k in range(KC):
                    mm(pbank[pj][0:32, :], WIN[:, k, LR0:LR0 + 32], xn[:, k, :], k == 0, k == KC - 1, [B["xn"]], [pbuf[pj]])
                P.act(lambda e, pj=pj: e.copy(lrT, pbank[pj][0:32, :]), [pbuf[pj]], [B["lrT"]])
                def rgg_groups(t=t, t0=t0):
                    for (col0, dst, dname) in ((R0, Rs, "R"), (GA0, GAs, "GA"), (GB0, GBs, "GB")):
                        for half in range(2):
                            ri = ringi[0] % 4
                            ringi[0] += 1
                            rb = ring[ri]
                            for j in range(4):
                                c = half * 4 + j
                                pj = next_pj()
                                for k in range(KC):
                                    mm(pbank[pj][:, :], WIN[:, k, col0 + c * 128:col0 + (c + 1) * 128], xn[:, k, :],
                                       k == 0, k == KC - 1, [B["xn"]], [pbuf[pj]])
                                P.act(lambda e, pj=pj, rb=rb, j=j: e.copy(rb[:, j, :], pbank[pj][:, :]),
                                      [pbuf[pj]], [B["ring%d" % ri]])
                                if j == 3:
                                    P.dma("F_ring%d" % ri,
                                          dst[half * 512:(half + 1) * 512, t0:t0 + 512].rearrange("(k p) n -> p k n", p=128), rb,
                                          reads=[B["ring%d" % ri]], writes=[DB(dname, t)])
                                yield
                rgg = rgg_groups()

                def rgg_step(n=1):
                    for _ in range(n):
                        next(rgg, None)
                for tc in range(4):
                    for cg in range(2):
                        pj = next_pj()
                        for k in range(KC):
                            mm(pbank[pj][:, :], xn[:, k, tc * 128:(tc + 1) * 128], WIN[:, k, V0 + cg * 512:V0 + (cg + 1) * 512],
                               k == 0, k == KC - 1, [B["xn"]], [pbuf[pj]])
                        P.dve(lambda e, pj=pj, tc=tc, cg=cg: e.tensor_copy(vtm[:, tc, cg * 512:(cg + 1) * 512], pbank[pj][:, :]),
                              [pbuf[pj]], [B["vtm"]])
                P.dma("F_vtm", Vs[t0:t0 + 512, :].rearrange("(c p) n -> p c n", p=128), vtm,
                      reads=[B["vtm"]], writes=[DB("V", t)])
                for tc in range(4):
                    pj = next_pj()
                    for k in range(KC):
                        mm(pbank[pj][:, :], xn[:, k, tc * 128:(tc + 1) * 128], WIN[:, k, U0:U0 + 512],
                           k == 0, k == KC - 1, [B["xn"]], [pbuf[pj]])
                    P.dve(lambda e, pj=pj, tc=tc: e.tensor_copy(utm[:, tc, :], pbank[pj][:, :]), [pbuf[pj]], [B["utm"]])
                P.dma("F_utm", Us[8 + t0:8 + t0 + 512, :].rearrange("(c p) n -> p c n", p=128), utm,
                      reads=[B["utm"]], writes=[DB("U", t)])
                def decays(h, A4, A4n):
                    for dr in range(2):
                        A_, Ai_ = A4[2 * dr], A4[2 * dr + 1]
                        An, Ain = A4n[2 * dr], A4n[2 * dr + 1]
                        mm(pbank[3][:, :], WLR[0:32, dr, h * 128:(h + 1) * 128], lrT[0:32, :], True, True, [B["lrT"]], [pbuf[3]])
                        nb_ = negb[:, (dr * L + l) * 4 + h:(dr * L + l) * 4 + h + 1]
                        P.act(lambda e, nb_=nb_: e.activation(el, pbank[3][:, :], AF.Exp, scale=-1.0, bias=nb_),
                              [pbuf[3], B_const], [B["el"]])
                        P.act(lambda e: e.activation(el, el, AF.Ln, bias=1.0), [B["el"]], [B["el"]])
                        for c in range(4):
                            if dr == 0:
                                P.dve(lambda e, c=c: e.tensor_tensor_scan(bc[:, c * 128:(c + 1) * 128], el[:, c * 128:(c + 1) * 128],
                                                                          zer, 0.0, ALU.add, ALU.add),
                                      [B["el"], B_const], [B["bc"]])
                            else:
                                P.dve(lambda e, c=c: e.tensor_tensor_scan(rev2(bc[:, c * 128:(c + 1) * 128]),
                                                                          rev2(el[:, c * 128:(c + 1) * 128]),
                                                                          zer, 0.0, ALU.add, ALU.add),
                                      [B["el"], B_const], [B["bc"]])
                        P.act(lambda e, A_=A_: e.activation(A_, bc, AF.Exp, scale=-1.0 / 16), [B["bc"]], [B[An]])
                        P.act(lambda e, Ai_=Ai_: e.activation(Ai_, bc, AF.Exp, scale=1.0 / 16), [B["bc"]], [B[Ain]])

                decays(0, Asets[0], Anames[0])
                for h in range(H):
                    Af, Aif, Ab, Aib = Asets[h % 2]
                    nAf, nAif, nAb, nAib = Anames[h % 2]
                    pq = next_pj()
                    for k in range(KC):
                        mm(pbank[pq][:, :], WIN[:, k, Q0 + h * 128:Q0 + (h + 1) * 128], xn[:, k, :], k == 0, k == KC - 1,
                           [B["xn"]], [pbuf[pq]])
                    pk = next_pj()
                    for k in range(KC):
                        mm(pbank[pk][:, :], WIN[:, k, K0 + h * 128:K0 + (h + 1) * 128], xn[:, k, :], k == 0, k == KC - 1,
                           [B["xn"]], [pbuf[pk]])
                    qi = h % 2
                    qb_ = qbs[qi]
                    P.dve(lambda e, pq=pq, Af=Af: e.tensor_tensor(qf, pbank[pq][:, :], Af, ALU.mult), [pbuf[pq], B[nAf]], [B["qf"]])
                    P.dve(lambda e, pq=pq, qb_=qb_, Ab=Ab: e.tensor_tensor(qb_, pbank[pq][:, :], Ab, ALU.mult),
                          [pbuf[pq], B[nAb]], [B["qbs%d" % qi]])
                    P.dve(lambda e, pk=pk, Aif=Aif: e.tensor_tensor(kf, pbank[pk][:, :], Aif, ALU.mult), [pbuf[pk], B[nAif]], [B["kf"]])
                    P.dve(lambda e, pk=pk, Aib=Aib: e.tensor_tensor(kbT, pbank[pk][:, :], Aib, ALU.mult), [pbuf[pk], B[nAib]], [B["kbT"]])
                    P.dma("F_qbs%d" % qi, QB[h * 128:(h + 1) * 128, t0:t0 + 512], qb_, reads=[B["qbs%d" % qi]], writes=[DB("QB", t)])
                    P.pool(lambda e, h=h, t=t, Ab=Ab: e.tensor_copy(abfirst[:, h, t * 4:(t + 1) * 4],
                                                              Ab.rearrange("p (c j) -> p c j", j=128)[:, :, 0]),
                           [B[nAb]], [B_abf])
                    oi = h % 2
                    trv = pbank[7].bitcast(BF16)
                    for c in range(4):
                        cs = slice(c * 128, (c + 1) * 128)
                        P.pe(lambda e, c=c, cs=cs: e.transpose(trv[:, c * 128:(c + 1) * 128], kf[:, cs], ident), [B["kf"], B_const], [pbuf[7]])
                        P.pe(lambda e, c=c, cs=cs: e.transpose(trv[:, 512 + c * 128:512 + (c + 1) * 128], kbT[:, cs], ident), [B["kbT"], B_const], [pbuf[7]])
                    P.act(lambda e: e.copy(ktm4, trv[:, 0:512].rearrange("p (c j) -> p c j", j=128)), [pbuf[7]], [B["ktm4"]])
                    P.act(lambda e, h=h: e.copy(kbstg[:, :, h * 128:(h + 1) * 128], trv[:, 512:1024].rearrange("p (c j) -> p c j", j=128)),
                          [pbuf[7]], [B["kbstg"]])
                    if h + 1 < H:
                        decays(h + 1, Asets[(h + 1) % 2], Anames[(h + 1) % 2])
                    for c in range(4):
                        cs = slice(c * 128, (c + 1) * 128)
                        si = c % 2
                        scb_ = pbuf[4]
                        mm(pbank[4][:, 0:128], kf[:, cs], qf[:, cs], True, True, [B["kf"], B["qf"]], [scb_])
                        mm(pbank[4][:, 128:256], kbT[:, cs], qb_[:, cs], True, True,
                           [B["kbT"], B["qbs%d" % qi]], [scb_])
                        P.dve(lambda e, si=si: e.tensor_tensor(sTf[si], pbank[4][:, 0:128], mask_f, ALU.mult),
                              [scb_, B_const], [B["sTf%d" % si]])
                        P.dve(lambda e, si=si: e.tensor_tensor(sTb[si], pbank[4][:, 128:256], mask_b, ALU.mult),
                              [scb_, B_const], [B["sTb%d" % si]])
                        rgg_step(1)
                        for vc in range(2):
                            vcol = h * 256 + vc * 128
                            ob = 5 + vc
                            mm(pbank[ob][:, cs], vtm[:, c, vcol:vcol + 128], sTf[si], True, False,
                               [B["vtm"], B["sTf%d" % si]], [pbuf[ob]])
                            mm(pbank[ob][:, cs], vtm[:, c, vcol:vcol + 128], sTb[si], False, False,
                               [B["vtm"], B["sTb%d" % si]], [pbuf[ob]])
                            mm(pbank[ob][:, cs], Sbf[:, h, vc * 128:(vc + 1) * 128], qf[:, cs], False, True,
                               [B_Sbf[h], B["qf"]], [pbuf[ob]])
                        mm(pbank[4][:, 256:512], ktm4[:, c, :], vtm[:, c, h * 256:(h + 1) * 256], True, True,
                           [B["ktm4"], B["vtm"]], [B_P4])
                        aprev = akeep[:, h:h + 1] if c == 0 else Af[:, (c - 1) * 128 + 127:(c - 1) * 128 + 128]
                        P.dve(lambda e, h=h, aprev=aprev: e.scalar_tensor_tensor(S32[:, h, :], S32[:, h, :], aprev, pbank[4][:, 256:512],
                                                                                 ALU.mult, ALU.add),
                              [B_P4, B_S32[h], B[nAf], B_akeep], [B_S32[h]])
                        alast = Af[:, c * 128 + 127:c * 128 + 128]
                        P.dve(lambda e, h=h, alast=alast: e.tensor_scalar(Sbf[:, h, :], S32[:, h, :], alast, None, ALU.mult),
                              [B_S32[h], B[nAf]], [B_Sbf[h]])
                        if c == 3:
                            P.dve(lambda e, h=h, alast=alast: e.tensor_copy(akeep[:, h:h + 1], alast), [B[nAf]], [B_akeep])
                        if c % 2 == 1:
                            rgg_step(1)
                    os_ = ost[oi]
                    P.act(lambda e, os_=os_: e.copy(os_[:, 0, :], pbank[5][:, :]), [pbuf[5]], [B["ost%d" % oi]])
                    P.dve(lambda e, os_=os_: e.tensor_copy(os_[:, 1, :], pbank[6][:, :]), [pbuf[6]], [B["ost%d" % oi]])
                    P.dma("F_ost%d" % oi, Os[h * 256:(h + 1) * 256, t0:t0 + 512].rearrange("(k p) n -> p k n", p=128), os_,
                          reads=[B["ost%d" % oi]], writes=[DB("O", t)])
                rgg_step(24)
                P.dma("F_kbstg", KB[t0:t0 + 512, :].rearrange("(c p) n -> p c n", p=128), kbstg,
                      reads=[B["kbstg"]], writes=[DB("KB", t)])
            P.barrier()

            ar.off = persist_end
            WBA = ar.alloc([KC, D], BF16)
            WBB = ar.alloc([4, D], BF16)
            WOUT = ar.alloc([KC, D], BF16)
            WGRP = ar.alloc([4, 128], BF16)
            bandc = ar.alloc([20, 128], BF16)
            bandh = ar.alloc([20, 128], BF16, parts=16)
            Sb32 = ar.alloc([H, DV], F32)
            Sbb = ar.alloc([H, DV], BF16)
            rt = ar.alloc([KC, 512], BF16)
            gat = ar.alloc([KC, 512], BF16)
            gbt = ar.alloc([KC, 512], BF16)
            sets = []
            for si_ in range(2):
                o_ = ar.off
                ot_ = ar.alloc([KC, 512], BF16)
                m2_ = ar.alloc([KC, 512], BF16, at=o_)
                o_ = ar.off
                qbt_ = ar.alloc([H, 512], BF16)
                pbf_ = ar.alloc([4, 512], BF16, at=o_)
                o_ = ar.off
                kbt_ = ar.alloc([4, 512], BF16)
                p2bf_ = ar.alloc([4, 512], BF16, at=o_)
                o_ = ar.off
                vt_ = ar.alloc([4, 1024], BF16)
                m1_ = ar.alloc([KC, 512], BF16, at=o_)
                sets.append(dict(ot=ot_, m2=m2_, qbt=qbt_, pbf=pbf_, kbt=kbt_, p2bf=p2bf_, vt=vt_, m1=m1_,
                                 Bot=Buf("B_ot%d" % si_), Bqbt=Buf("B_qbt%d" % si_), Bkbt=Buf("B_kbt%d" % si_), Bvt=Buf("B_vt%d" % si_)))
            ut = ar.alloc([4, 512], BF16)
            uh = ar.alloc([4, 512], BF16, parts=16)
            xtb = ar.alloc([KC, 512], F32)
            stg_off = ar.off
            osum = ar.alloc([KC, 512], F32)
            stgb = [ar.alloc([2048], F32, at=stg_off), ar.alloc([2048], F32, at=stg_off + 8192)]
            sqy = ar.alloc([KC, 512], BF16)
            rstd4 = ar.alloc([H, 512], F32)
            lnb = ar.alloc([512], F32)
            B = {n: Buf("B_" + n) for n in ("rt gat gbt ut uh xtb osum sqy rstd4 lnb stg0 stg1").split()}
            B_Sb32 = [Buf("Sb32_%d" % h) for h in range(H)]
            B_Sbb = [Buf("Sbb_%d" % h) for h in range(H)]
            stgB = [B["stg0"], B["stg1"]]
            load_w(WBA, w_ba[l * D:(l + 1) * D, :], KC, D, scale_col=SVO["og"] + 8 * l, stg=stgb, stgB=stgB)
            load_w(WBB, w_bb[l * 512:(l + 1) * 512, :], 4, D, scale_col=SVO["ps"] + 4 * l, stg=stgb, stgB=stgB)
            load_w(WOUT, w_out[l * D:(l + 1) * D, :], KC, D, stg=stgb, stgB=stgB)
            load_w(WGRP, w_grp[l * 512:(l + 1) * 512, :], 4, 128, stg=stgb, stgB=stgB)
            bsrc = bands.rearrange("(m r) n -> r m n", r=144)
            for half in range(2):
                i = rr[0] % 2
                rr[0] += 1
                s3 = stgb[i][:, 0:1280].rearrange("p (m n) -> p m n", n=128)
                P.dma("stg%d" % i, s3, bsrc[0:128, half * 10:(half + 1) * 10, :], writes=[stgB[i]])
                P.dve(lambda e, s3=s3, half=half: e.tensor_copy(bandc[:, half * 10:(half + 1) * 10, :], s3), [stgB[i]], [])
                i = rr[0] % 2
                rr[0] += 1
                s3h = stgb[i][0:16, 0:1280].rearrange("p (m n) -> p m n", n=128)
                P.dma("stg%d" % i, s3h, bsrc[128:144, half * 10:(half + 1) * 10, :], writes=[stgB[i]])
                P.dve(lambda e, s3h=s3h, half=half: e.tensor_copy(bandh[:, half * 10:(half + 1) * 10, :], s3h), [stgB[i]], [])
            P.barrier()
            pjr = [0]

            def next_pj():
                i = pjr[0] % 3
                pjr[0] += 1
                return i

            def load_set(tt, S_):
                tt0 = tt * 512
                P.dma("B_qbt%d" % S_["i"], S_["qbt"], QB[:, tt0:tt0 + 512].rearrange("(k p) n -> p k n", p=128), reads=[DB("QB", tt)], writes=[S_["Bqbt"]])
                P.dma("B_kbt%d" % S_["i"], S_["kbt"], KB[tt0:tt0 + 512, :].rearrange("(c p) n -> p c n", p=128), reads=[DB("KB", tt)], writes=[S_["Bkbt"]])
                P.dma("B_vt%d" % S_["i"], S_["vt"], Vs[tt0:tt0 + 512, :].rearrange("(c p) n -> p c n", p=128), reads=[DB("V", tt)], writes=[S_["Bvt"]])
                P.dma("B_ot%d" % S_["i"], S_["ot"], Os[:, tt0:tt0 + 512].rearrange("(k p) n -> p k n", p=128), reads=[DB("O", tt)], writes=[S_["Bot"]])

            sets[0]["i"] = 0
            sets[1]["i"] = 1
            load_set(NTF - 1, sets[0])
            for t in range(NTF - 1, -1, -1):
                t0 = t * 512
                S_ = sets[(NTF - 1 - t) % 2]
                ot, m2, qbt, pbf, kbt, p2bf, vt, m1 = (S_[k_] for k_ in ("ot", "m2", "qbt", "pbf", "kbt", "p2bf", "vt", "m1"))
                B["ot"] = B["m2"] = S_["Bot"]
                B["qbt"] = B["pbf"] = S_["Bqbt"]
                B["kbt"] = B["p2bf"] = S_["Bkbt"]
                B["vt"] = B["m1"] = S_["Bvt"]
                if t > 0:
                    load_set(t - 1, sets[(NTF - t) % 2])
                if t == NTF - 1:
                    for h in range(H):
                        P.dve(lambda e, h=h: e.memset(Sb32[:, h, :], 0.0), [], [B_Sb32[h]])
                        P.dve(lambda e, h=h: e.memset(Sbb[:, h, :], 0.0), [], [B_Sbb[h]])
                elif (t0 + 512) % UNIT == 0:
                    for h in range(H):
                        P.dve(lambda e, h=h: e.tensor_scalar(Sb32[:, h, :], Sb32[:, h, :], flag[:, 0:1], None, ALU.mult),
                              [B_Sb32[h], B_const], [B_Sb32[h]])
                        P.dve(lambda e, h=h: e.tensor_scalar(Sbb[:, h, :], Sbb[:, h, :], flag[:, 0:1], None, ALU.mult),
                              [B_Sbb[h], B_const], [B_Sbb[h]])
                fm = lambda T_: T_[:, t0:t0 + 512].rearrange("(k p) n -> p k n", p=128)
                P.dma("B_rt", rt, fm(Rs), reads=[DB("R", t)], writes=[B["rt"]])
                P.dma("B_gat", gat, fm(GAs), reads=[DB("GA", t)], writes=[B["gat"]])
                P.dma("B_gbt", gbt, fm(GBs), reads=[DB("GB", t)], writes=[B["gbt"]])
                P.dma("B_ut", ut, Us[8 + t0:8 + t0 + 512, :].rearrange("(c p) n -> p c n", p=128), reads=[DB("U", t)], writes=[B["ut"]])
                P.dma("B_uh", uh[0:8], Us[t0:t0 + 512, :].rearrange("(c p) n -> p c n", p=128)[0:8],
                      reads=[DB("U", t - 1), DB("U", t)], writes=[B["uh"]])
                uh2src = AP(Us.tensor, Us.offset + (8 + t0 + 128) * 512, [[512, 8], [128 * 512, 4], [1, 512]])
                P.dma("B_uh2", uh[8:16], uh2src,
                      reads=[DB("U", t + 1), DB("U", t)], writes=[B["uh"]])
                P.dma("B_xtb", xtb, fm(Xsrc), reads=[DB("X", t)], writes=[B["xtb"]])
                P.act(lambda e: e.activation(sqy, rt, AF.Sigmoid), [B["rt"]], [B["sqy"]])
                P.pool(lambda e: e.tensor_tensor(rt, rt, sqy, ALU.mult), [B["rt"], B["sqy"]], [B["rt"]])
                P.act(lambda e: e.activation(gat, gat, AF.Sigmoid), [B["gat"]], [B["gat"]])
                P.act(lambda e: e.activation(gbt, gbt, AF.Sigmoid), [B["gbt"]], [B["gbt"]])
                osum4 = osum.rearrange("p (h v) n -> p h v n", v=2)
                ot4 = ot.rearrange("p (h v) n -> p h v n", v=2)
                for c in range(3, -1, -1):
                    cs = slice(c * 128, (c + 1) * 128)
                    obk = (5, 6) if c % 2 == 1 else (0, 1)
                    gck = t * 4 + c
                    gprev = min(gck + 1, NCK - 1)
                    for h in range(H):
                        for vc in range(2):
                            mm(pbank[obk[vc]][:, h * 128:(h + 1) * 128], Sbb[:, h, vc * 128:(vc + 1) * 128], qbt[:, h, cs], True, True,
                               [B_Sbb[h], B["qbt"]], [pbuf[obk[vc]]])
                        pb_, pc_ = (7, (h % 2) * 256) if h < 2 else (4, (h % 2) * 256)
                        mm(pbank[pb_][:, pc_:pc_ + 256], kbt[:, c, h * 128:(h + 1) * 128], vt[:, c, h * 256:(h + 1) * 256], True, True,
                           [B["kbt"], B["vt"]], [pbuf[pb_]])
                        aprev = abfirst[:, h, gprev:gprev + 1]
                        P.dve(lambda e, h=h, aprev=aprev, pb_=pb_, pc_=pc_: e.scalar_tensor_tensor(
                            Sb32[:, h, :], Sb32[:, h, :], aprev, pbank[pb_][:, pc_:pc_ + 256], ALU.mult, ALU.add),
                              [pbuf[pb_], B_Sb32[h], B_abf], [B_Sb32[h]])
                        afirst = abfirst[:, h, gck:gck + 1]
                        P.dve(lambda e, h=h, afirst=afirst: e.tensor_scalar(Sbb[:, h, :], Sb32[:, h, :], afirst, None, ALU.mult),
                              [B_Sb32[h], B_abf], [B_Sbb[h]])
                    for vc in range(2):
                        P.dve(lambda e, vc=vc, cs=cs, obk=obk, ot4=ot4: e.tensor_tensor(
                            osum4[:, :, vc, cs], pbank[obk[vc]][:, :].rearrange("p (h j) -> p h j", j=128), ot4[:, :, vc, cs], ALU.add),
                              [pbuf[obk[vc]], B["ot"]], [B["osum"]])
                def pool_branch():
                    for g in range(4):
                        for c in range(4):
                            gck = t * 4 + c
                            pos = gck % (UNIT // 128)
                            if pos == 0:
                                cls = 0 if gck == 0 else 2
                            elif pos == UNIT // 128 - 1:
                                cls = 1 if gck < UNIT // 128 else 3
                            else:
                                cls = 4
                            cs = slice(c * 128, (c + 1) * 128)
                            mm(pbank[4][:, cs], ut[:, c, g * 128:(g + 1) * 128], bandc[:, cls * 4 + g, :], True, False, [B["ut"]], [pbuf[4]])
                            mm(pbank[4][:, cs], uh[0:16, c, g * 128:(g + 1) * 128], bandh[0:16, cls * 4 + g, :], False, True, [B["uh"]], [pbuf[4]])
                        P.act(lambda e, g=g, pbf=pbf: e.copy(pbf[:, g, :], pbank[4][:, :]), [pbuf[4]], [B["pbf"]])
                        pj = next_pj()
                        mm(pbank[pj][:, :], WGRP[:, g, :], pbf[:, g, :], True, True, [B["pbf"]], [pbuf[pj]])
                        P.act(lambda e, g=g, pj=pj, p2bf=p2bf: e.copy(p2bf[:, g, :], pbank[pj][:, :]), [pbuf[pj]], [B["p2bf"]])
                    for oc in range(KC):
                        pj = next_pj()
                        for g in range(4):
                            mm(pbank[pj][:, :], WBB[:, g, oc * 128:(oc + 1) * 128], p2bf[:, g, :], g == 0, g == 3, [B["p2bf"]], [pbuf[pj]])
                        P.dve(lambda e, pj=pj, oc=oc, m2=m2: e.tensor_tensor(m2[:, oc, :], pbank[pj][:, :], gbt[:, oc, :], ALU.mult),
                              [pbuf[pj], B["gbt"]], [B["m2"]])
                P.act(lambda e: e.activation(sqy, osum, AF.Square), [B["osum"]], [B["sqy"]])
                pool_branch()
                for h in range(H):
                    for vc in range(2):
                        mm(pbank[3][:, :], ones, sqy[:, h * 2 + vc, :], vc == 0, vc == 1, [B["sqy"], B_const], [pbuf[3]])
                    P.act(lambda e: e.activation(lnb, pbank[3][:, :], AF.Ln, scale=1.0 / DV, bias=EPS), [pbuf[3]], [B["lnb"]])
                    P.act(lambda e, h=h: e.activation(rstd4[:, h, :], lnb, AF.Exp, scale=-0.5), [B["lnb"]], [B["rstd4"]])
                P.dve(lambda e: e.tensor_tensor(osum4, osum4, rstd4.unsqueeze(2).broadcast_to([128, H, 2, 512]), ALU.mult),
                      [B["osum"], B["rstd4"]], [B["osum"]])
                P.dve(lambda e: e.tensor_tensor(sqy, osum, rt, ALU.mult), [B["osum"], B["rt"], B["sqy"]], [B["sqy"]])
                for oc in range(KC):
                    pj = next_pj()
                    for k in range(KC):
                        mm(pbank[pj][:, :], WBA[:, k, oc * 128:(oc + 1) * 128], sqy[:, k, :], k == 0, k == KC - 1, [B["sqy"]], [pbuf[pj]])
                    P.dve(lambda e, pj=pj, oc=oc, m1=m1: e.tensor_tensor(m1[:, oc, :], pbank[pj][:, :], gat[:, oc, :], ALU.mult),
                          [pbuf[pj], B["gat"]], [B["m1"]])
                P.dve(lambda e, m1=m1, m2=m2: e.tensor_tensor(m1, m1, m2, ALU.add), [B["m1"], B["m2"]], [B["m1"]])
                for oc in range(KC):
                    pj = next_pj()
                    for k in range(KC):
                        mm(pbank[pj][:, :], WOUT[:, k, oc * 128:(oc + 1) * 128], m1[:, k, :], k == 0, k == KC - 1, [B["m1"]], [pbuf[pj]])
                    P.dve(lambda e, pj=pj, oc=oc: e.tensor_tensor(xtb[:, oc, :], pbank[pj][:, :], xtb[:, oc, :], ALU.add),
                          [pbuf[pj], B["xtb"]], [B["xtb"]])
                P.dma("B_xtb_st", XM[:, 1 + t0:1 + t0 + 512].rearrange("(k p) n -> p k n", p=128), xtb,
                      reads=[B["xtb"]], writes=[DB("XM", t)])
            P.barrier()

            ar.off = c_start
            WUP = ar.alloc([KC, 2 * DFF], BF16)
            WDN = ar.alloc([FC, D], BF16)
            WM = 510
            WM2 = WM + 2
            xm = ar.alloc([KC, WM2], F32)
            r16_off = ar.off
            xn2 = ar.alloc([KC, WM2], BF16)
            lnc = ar.alloc([WM2], F32)
            rsc = ar.alloc([WM2], F32)
            yc = [ar.alloc([WM2], F32) for _ in range(2)]
            assert ar.off - r16_off == 16384
            xres = ar.alloc([KC, WM2], F32, at=r16_off)
            sc_ = [ar.alloc([WM], BF16) for _ in range(2)]
            sqc = ar.alloc([KC, WM2], BF16)
            stg_off = ar.off
            h2 = ar.alloc([FC, WM2], BF16)
            stgc = [ar.alloc([2048], F32, at=stg_off), ar.alloc([2048], F32, at=stg_off + 8192)]
            assert stg_off + 16384 <= ar.off
            fl32 = ar.alloc([WM2], F32, at=stg_off)
            fr32 = ar.alloc([WM2], F32, at=stg_off + 2048)
            B = {n: Buf("C_" + n) for n in ("xm xn2 lnc rsc yc0 yc1 sc0 sc1 h2 sqc stg0 stg1").split()}
            R16B = [B["xn2"], B["lnc"], B["rsc"], B["yc0"], B["yc1"]]
            stgB = [B["stg0"], B["stg1"]]
            load_w(WUP, w_up[l * D:(l + 1) * D, :], KC, 2 * DFF, scale_col=SVO["g2"] + 8 * l, stg=stgc, stgB=stgB)
            load_w(WDN, w_dn[l * DFF:(l + 1) * DFF, :], FC, D, stg=stgc, stgB=stgB)
            P.barrier()
            pjr = [0]
            CR = [0, 1, 2, 4, 5, 6, 7]
            Xdst = Y if last_layer else Xs
            ctiles = []
            for u in range(2):
                c = 0
                while c < UNIT:
                    w = min(WM, UNIT - c)
                    ctiles.append((u * UNIT + c, w))
                    c += w

            def xm_reads(c0, W):
                ta, tb = max(c0 - 1, 0) // 512, min(c0 + W, NT - 1) // 512
                rd = [DB("XM", i) for i in range(ta, tb + 1)]
                if c0 == 0:
                    rd.append(DB("XM", -1))
                if c0 + W == NT:
                    rd.append(DB("XM", NTF))
                return rd

            def load_xm(c0, W):
                P.dma("C_xm", xm[:, :, 0:W + 2], XM[:, c0:c0 + W + 2].rearrange("(k p) n -> p k n", p=128),
                      reads=xm_reads(c0, W), writes=[B["xm"]])

            load_xm(*ctiles[0])
            for ti, (c0, W) in enumerate(ctiles):
                W2 = W + 2
                P.act(lambda e, W2=W2: e.activation(sqc[:, :, 0:W2], xm[:, :, 0:W2], AF.Square), [B["xm"]], [B["sqc"]])
                for k in range(KC):
                    mm(pbank[3][:, 0:W2], ones, sqc[:, k, 0:W2], k == 0, k == KC - 1, [B["sqc"], B_const], [pbuf[3]])
                P.act(lambda e, W2=W2: e.activation(lnc[:, 0:W2], pbank[3][:, 0:W2], AF.Ln, scale=1.0 / D, bias=EPS), [pbuf[3]], [B["lnc"]])
                P.act(lambda e, W2=W2: e.activation(rsc[:, 0:W2], lnc[:, 0:W2], AF.Exp, scale=-0.5), [B["lnc"]], [B["rsc"]])
                P.dve(lambda e, W2=W2: e.tensor_tensor(xn2[:, :, 0:W2], xm[:, :, 0:W2],
                                                        rsc[:, 0:W2].unsqueeze(1).broadcast_to([128, KC, W2]), ALU.mult),
                      [B["xm"], B["rsc"]], [B["xn2"]])
                if ti + 1 < len(ctiles):
                    load_xm(*ctiles[ti + 1])
                left_flag = (c0 == UNIT)
                right_flag = (c0 + W == UNIT)
                pend_h2 = None
                for f in range(FC):
                    ai = f % 2
                    pa = CR[pjr[0] % 7]
                    pjr[0] += 1
                    for k in range(KC):
                        mm(pbank[pa][:, 0:W2], WUP[:, k, f * 128:(f + 1) * 128], xn2[:, k, 0:W2], k == 0, k == KC - 1, [B["xn2"]], [pbuf[pa]])
                    pv = CR[pjr[0] % 7]
                    pjr[0] += 1
                    for k in range(KC):
                        mm(pbank[pv][:, 0:W], WUP[:, k, DFF + f * 128:DFF + (f + 1) * 128], xn2[:, k, 1:1 + W], k == 0, k == KC - 1,
                           [B["xn2"]], [pbuf[pv]])
                    cwo = SVO["cw"] + l * 3 * FC
                    w0 = sv[:, cwo + f:cwo + f + 1]
                    w1 = sv[:, cwo + FC + f:cwo + FC + f + 1]
                    w2 = sv[:, cwo + 2 * FC + f:cwo + 2 * FC + f + 1]
                    cbv = sv[:, SVO["cb"] + l * FC + f:SVO["cb"] + l * FC + f + 1]
                    if left_flag:
                        P.dve(lambda e, pa=pa: e.tensor_scalar(pbank[pa][:, 0:1], pbank[pa][:, 0:1], flag[:, 0:1], None, ALU.mult),
                              [pbuf[pa], B_const], [pbuf[pa]])
                    if right_flag:
                        P.dve(lambda e, pa=pa, W2=W2: e.tensor_scalar(pbank[pa][:, W2 - 1:W2], pbank[pa][:, W2 - 1:W2], flag[:, 0:1], None, ALU.mult),
                              [pbuf[pa], B_const], [pbuf[pa]])
                    P.dve(lambda e, pa=pa, ai=ai, W=W, w0=w0, cbv=cbv: e.tensor_scalar(yc[ai][:, 0:W], pbank[pa][:, 0:W], w0, cbv, ALU.mult, ALU.add),
                          [pbuf[pa], B_const], [B["yc%d" % ai]])
                    P.dve(lambda e, pa=pa, ai=ai, W=W, w1=w1: e.scalar_tensor_tensor(yc[ai][:, 0:W], pbank[pa][:, 1:1 + W], w1, yc[ai][:, 0:W], ALU.mult, ALU.add),
                          [pbuf[pa], B["yc%d" % ai], B_const], [B["yc%d" % ai]])
                    P.dve(lambda e, pa=pa, ai=ai, W=W, w2=w2: e.scalar_tensor_tensor(yc[ai][:, 0:W], pbank[pa][:, 2:2 + W], w2, yc[ai][:, 0:W], ALU.mult, ALU.add),
                          [pbuf[pa], B["yc%d" % ai], B_const], [B["yc%d" % ai]])
                    P.act(lambda e, ai=ai, W=W: e.activation(sc_[ai][:, 0:W], yc[ai][:, 0:W], AF.Silu), [B["yc%d" % ai]], [B["sc%d" % ai]])
                    if pend_h2 is not None:
                        pend_h2()
                    pend_h2 = (lambda ai=ai, pv=pv, f=f, W=W: P.dve(
                        lambda e: e.tensor_tensor(h2[:, f, 0:W], pbank[pv][:, 0:W], sc_[ai][:, 0:W], ALU.mult),
                        [pbuf[pv], B["sc%d" % ai]], [B["h2"]]))
                pend_h2()
                P.dma("C_xres", xres[:, :, 0:W], XM[:, c0 + 1:c0 + 1 + W].rearrange("(k p) n -> p k n", p=128),
                      reads=xm_reads(c0, W), writes=R16B)
                for oc in range(KC):
                    pj = CR[pjr[0] % 7]
                    pjr[0] += 1
                    for f in range(FC):
                        mm(pbank[pj][:, 0:W], WDN[:, f, oc * 128:(oc + 1) * 128], h2[:, f, 0:W], f == 0, f == FC - 1, [B["h2"]], [pbuf[pj]])
                    rb_ = R16B[min(max(oc - 3, 0), 4)] if not last_layer else None
                    P.dve(lambda e, pj=pj, oc=oc, W=W: e.tensor_tensor(xres[:, oc, 0:W], pbank[pj][:, 0:W], xres[:, oc, 0:W], ALU.add),
                          [pbuf[pj]] + (R16B if last_layer else [rb_]), R16B if last_layer else [rb_])
                    if not last_layer:
                        P.dma("C_xo%d" % oc, Xdst[oc * 128:(oc + 1) * 128, c0:c0 + W], xres[:, oc, 0:W], reads=[rb_],
                              writes=[DB("X", i) for i in range(c0 // 512, (c0 + W - 1) // 512 + 1)])
                if last_layer:
                    P.act(lambda e, W=W: e.activation(sqc[:, :, 0:W], xres[:, :, 0:W], AF.Square), R16B, [B["sqc"]])
                    for k in range(KC):
                        mm(pbank[3][:, 0:W], ones, sqc[:, k, 0:W], k == 0, k == KC - 1, [B["sqc"], B_const], [pbuf[3]])
                    P.act(lambda e, W=W: e.activation(fl32[:, 0:W], pbank[3][:, 0:W], AF.Ln, scale=1.0 / D, bias=EPS),
                          [pbuf[3]], [B["h2"]])
                    P.act(lambda e, W=W: e.activation(fr32[:, 0:W], fl32[:, 0:W], AF.Exp, scale=-0.5), [B["h2"]], [B["h2"]])
                    for k in range(KC):
                        fgk = sv[:, SVO["fg"] + k:SVO["fg"] + k + 1]
                        P.dve(lambda e, k=k, fgk=fgk, W=W: e.scalar_tensor_tensor(xres[:, k, 0:W], xres[:, k, 0:W], fgk, fr32[:, 0:W], ALU.mult, ALU.mult),
                              R16B + [B["h2"], B_const], R16B)
                if last_layer:
                    wr = [DB("Y", i) for i in range(c0 // 512, (c0 + W - 1) // 512 + 1)]
                    P.dma("C_xo", Xdst[:, c0:c0 + W].rearrange("(k p) n -> p k n", p=128), xres[:, :, 0:W], reads=R16B, writes=wr)
            P.barrier()
        P.add("sp", None, reads=[DB("Y", t) for t in range(NTF)])
        stats = P.emit(st)
    return nc, stats


def host_inputs(inp, cfg, x_units, flags):
    L = cfg["DEPTH"]
    UNIT = cfg["UNIT"]
    svec = small_layout(inp, L)
    eye = np.eye(128, dtype=np.float32)
    jj, ii = np.meshgrid(np.arange(128), np.arange(128), indexing="ij")
    cst = np.concatenate([eye, np.ones((128, 128), np.float32), (jj <= ii).astype(np.float32), (jj >= ii).astype(np.float32)], axis=1)
    bE, bF, bL = band_matrices("E"), band_matrices("first"), band_matrices("last")
    wlr = np.zeros((L, 32, 2, 512), np.float32)
    wlr[:, 0:16, 0, :] = inp["w_lr2_f"]
    wlr[:, 16:32, 1, :] = inp["w_lr2_b"]
    shared = dict(
        svec=svec, cst=np.ascontiguousarray(cst),
        w_in=np.ascontiguousarray(inp["w_in"]).reshape(L * D, INW),
        wlr=wlr.reshape(L * 32, 1024),
        w_grp=np.ascontiguousarray(inp["w_pool_grp"]).reshape(L * 512, 128),
        w_ba=np.ascontiguousarray(inp["w_br_a"]).reshape(L * D, D),
        w_bb=np.ascontiguousarray(inp["w_br_b"]).reshape(L * 512, D),
        w_out=np.ascontiguousarray(inp["w_out"]).reshape(L * D, D),
        w_up=np.ascontiguousarray(inp["w_up"]).reshape(L * D, 2 * DFF),
        w_dn=np.ascontiguousarray(inp["w_down"]).reshape(L * DFF, D),
    )
    maps = []
    for units, fl in zip(x_units, flags):
        xs = [u if u is not None else np.zeros((UNIT, D), np.float32) for u in units]
        xT = np.ascontiguousarray(np.concatenate(xs, axis=0).T)
        bands = np.stack([bF, bE if fl else bL, bE if fl else bF, bL, bE], axis=0)
        m = dict(shared)
        m["xT"] = xT
        m["bands"] = np.ascontiguousarray(bands.reshape(5 * 4 * 144, 128))
        m["flag"] = np.full((128, 1), 1.0 if fl else 0.0, np.float32)
        maps.append(m)
    return maps


_PROG_CACHE = {}


def kernel(**inputs):
    cfg = dict(CFG)
    inp = {k: np.asarray(v, dtype=np.float32) for k, v in inputs.items()}
    UNIT = cfg["UNIT"]
    xp = inp["x_prompt"][0]
    xsmp = inp["x_sample"]
    x_units = [[xp[0:UNIT], xp[UNIT:2 * UNIT]], [xsmp[0], xsmp[1]]] + [[xsmp[i], None] for i in range(2, 8)]
    flags = [True] + [False] * 7
    maps = host_inputs(inp, cfg, x_units, flags)
    key = (cfg["UNIT"], cfg["DEPTH"])
    if key not in _PROG_CACHE:
        _PROG_CACHE[key] = build_program(cfg)[0]
    nc = _PROG_CACHE[key]
    res = run_bass_kernel_spmd(nc, maps, core_ids=list(range(8)))
    ys = [np.asarray(r["Y"]) for r in res.results]
    y_prompt = np.ascontiguousarray(ys[0].T)[None]
    y_sample = np.empty_like(xsmp)
    y_sample[0] = ys[1][:, 0:UNIT].T
    y_sample[1] = ys[1][:, UNIT:2 * UNIT].T
    for i in range(2, 8):
        y_sample[i] = ys[i][:, 0:UNIT].T
    return (y_prompt.astype(np.float32), y_sample.astype(np.float32))
```

```python
import numpy as np
from contextlib import ExitStack
import concourse.bass as bass
import concourse.mybir as mybir
from concourse.bass_utils import run_bass_kernel_spmd
from concourse.ap import AP

F32 = mybir.dt.float32
BF16 = mybir.dt.bfloat16
ALU = mybir.AluOpType
AF = mybir.ActivationFunctionType

D = 1024
KC = 8
H = 4
DK = 128
DV = 256
INW = 5664
DFF = 2816
FC = 22
EPS = 1e-6
Q0, K0, V0, R0, LR0, U0, GA0, GB0 = 0, 512, 1024, 2048, 3072, 3104, 3616, 4640
POOL_WINDOWS = (2, 4, 8, 16)

CFG = dict(UNIT=8192, DEPTH=4, NCORES=8)


class Buf:
    __slots__ = ("name", "last_w", "readers")

    def __init__(self, name):
        self.name = name
        self.last_w = None
        self.readers = []


class Op:
    __slots__ = ("eng", "fn", "reads", "writes", "dma", "key", "kidx", "waits", "signal", "value", "barrier")

    def __init__(self, eng, fn, reads, writes, dma):
        self.eng = eng
        self.fn = fn
        self.reads = reads
        self.writes = writes
        self.dma = dma
        self.key = None
        self.kidx = 0
        self.waits = []
        self.signal = False
        self.value = 0
        self.barrier = False


ENGS = ("pe", "dve", "act", "pool", "sp")


class Prog:
    def __init__(self, nc):
        self.nc = nc
        self.ops = []
        self.kcount = {}

    def add(self, eng, fn, reads=(), writes=(), dma=None):
        op = Op(eng, fn, tuple(reads), tuple(writes), dma)
        op.key = ("dma:" + dma) if dma is not None else eng
        c = self.kcount.get(op.key, 0) + 1
        self.kcount[op.key] = c
        op.kidx = c
        self.ops.append(op)
        return op

    def pe(self, fn, reads=(), writes=()):
        return self.add("pe", fn, reads, writes)

    def dve(self, fn, reads=(), writes=()):
        return self.add("dve", fn, reads, writes)

    def act(self, fn, reads=(), writes=()):
        return self.add("act", fn, reads, writes)

    def pool(self, fn, reads=(), writes=()):
        return self.add("pool", fn, reads, writes)

    def dma(self, group, out, in_, reads=(), writes=(), slow=False):
        if slow:
            return self.add("sp", lambda e: e.dma_start(out=out, in_=in_, allow_slow_non_contiguous=True), reads, writes, dma=group)
        return self.add("sp", lambda e: e.dma_start(out=out, in_=in_), reads, writes, dma=group)

    def barrier(self):
        op = Op("sp", None, (), (), None)
        op.barrier = True
        self.ops.append(op)

    def analyze(self):
        known = {e: {} for e in ENGS}
        last_by_key = {}
        pending = {e: None for e in ENGS}
        for op in self.ops:
            if op.barrier:
                snap = dict(last_by_key)
                for e in ENGS:
                    pending[e] = snap
                continue
            deps = {}

            def need(p, kind):
                if p is None or p is op:
                    return
                if p.dma is None and op.dma is None and p.eng == op.eng:
                    if op.eng == "pe" or op.eng == "sp":
                        return
                    if kind != "raw":
                        return
                cur = deps.get(p.key)
                if cur is None or p.kidx > cur.kidx:
                    deps[p.key] = p

            for b in op.reads:
                need(b.last_w, "raw")
            for b in op.writes:
                need(b.last_w, "waw")
                for r in b.readers:
                    need(r, "war")
            if pending[op.eng] is not None:
                for k, p in pending[op.eng].items():
                    if p.dma is None and p.eng == op.eng and op.dma is None:
                        continue
                    cur = deps.get(k)
                    if cur is None or p.kidx > cur.kidx:
                        deps[k] = p
                pending[op.eng] = None
            kn = known[op.eng]
            for k, p in deps.items():
                if kn.get(k, 0) >= p.kidx:
                    continue
                kn[k] = p.kidx
                p.signal = True
                op.waits.append(p)
            for b in op.writes:
                b.last_w = op
                b.readers = []
            for b in op.reads:
                b.readers.append(op)
            if op.fn is not None:
                last_by_key[op.key] = op
        cnt = {}
        for op in self.ops:
            if op.signal:
                c = cnt.get(op.key, 0) + 1
                cnt[op.key] = c
                op.value = c * (16 if op.dma is not None else 1)
        self.sigcount = cnt

    def emit(self, stack):
        nc = self.nc
        self.analyze()
        sems = {}
        for k in self.sigcount:
            sems[k] = stack.enter_context(nc.semaphore("s_" + k.replace(":", "_")))
        per = {e: [] for e in ENGS}
        for op in self.ops:
            if not op.barrier:
                per[op.eng].append(op)
        block = stack.enter_context(nc.Block())

        def run(eng_name):
            def body(eng):
                for op in per[eng_name]:
                    for p in op.waits:
                        eng.wait_ge(sems[p.key], p.value)
                    if op.fn is not None:
                        ins = op.fn(eng)
                        if op.signal:
                            ins.then_inc(sems[op.key], 16 if op.dma is not None else 1)
            return body

        block.tensor(run("pe"))
        block.vector(run("dve"))
        block.scalar(run("act"))
        block.gpsimd(run("pool"))
        block.sync(run("sp"))
        return {k: len(v) for k, v in per.items()}, len(sems)


class Arena:
    def __init__(self, nc, stack, nbytes):
        self.t = stack.enter_context(nc.sbuf_tensor("arena", [128, nbytes // 2], BF16))
        self.cap = nbytes
        self.off = 0

    def alloc(self, shape, dt, parts=128, at=None):
        n = int(np.prod(shape))
        esz = 2 if dt == BF16 else 4
        nb = (n * esz + 63) // 64 * 64
        if at is None:
            o = self.off
            self.off += nb
            assert self.off <= self.cap, ("SBUF arena overflow", self.off, self.cap)
        else:
            o = at
        v = self.t[0:parts, o // 2: o // 2 + n * esz // 2]
        if dt == F32:
            v = v.bitcast(F32)
        if len(shape) == 2:
            v = v.rearrange("p (a b) -> p a b", a=shape[0])
        elif len(shape) == 3:
            v = v.rearrange("p (a b c) -> p a b c", a=shape[0], b=shape[1])
        return v


def rev2(v):
    (ps, pc), (es, en) = v.ap
    return AP(v.tensor, v.offset + (en - 1) * es, [[ps, pc], [-es, en]])


def small_layout(inp, L):
    def pp(a, nch):
        return np.ascontiguousarray(a.reshape(L, nch, 128).transpose(2, 0, 1).reshape(128, L * nch))
    parts = [pp(inp["norm1_g"], 8), pp(inp["norm2_g"], 8), pp(inp["onorm_g"], 8), pp(inp["pool_scale"], 4),
             pp(inp["b_lr_f"], 4), pp(inp["b_lr_b"], 4),
             np.ascontiguousarray(inp["conv_w"].reshape(L, 3, FC, 128).transpose(3, 0, 1, 2).reshape(128, L * 3 * FC)),
             pp(inp["conv_b"], FC),
             np.ascontiguousarray(inp["final_g"].reshape(8, 128).T)]
    return np.concatenate(parts, axis=1).astype(np.float32)


def sv_offsets(L):
    o = {}
    c = 0
    for name, n in [("g1", 8 * L), ("g2", 8 * L), ("og", 8 * L), ("ps", 4 * L), ("bf", 4 * L), ("bb", 4 * L),
                    ("cw", 3 * FC * L), ("cb", FC * L), ("fg", 8)]:
        o[name] = c
        c += n
    o["_n"] = c
    return o


def band_matrices(cls):
    out = np.zeros((4, 144, 128), np.float32)
    for gi, w in enumerate(POOL_WINDOWS):
        for t in range(128):
            lo = t - w // 2
            hi = t + (w - 1 - w // 2)
            if cls == "first":
                lo = max(lo, 0)
            if cls == "last":
                hi = min(hi, 127)
            cnt = hi - lo + 1
            for tp in range(lo, hi + 1):
                if tp < 0:
                    row = 128 + (tp + 8)
                elif tp > 127:
                    row = 136 + (tp - 128)
                else:
                    row = tp
                out[gi, row, t] += 1.0 / cnt
            out[gi, t, t] -= 1.0
    return out


def build_program(cfg):
    UNIT = cfg["UNIT"]
    L = cfg["DEPTH"]
    NT = 2 * UNIT
    NTF = NT // 512
    NCK = NT // 128
    TC = 256
    NTC = NT // TC
    SVO = sv_offsets(L)
    NS = SVO["_n"]

    nc = bass.Bass("TRN2", target_bir_lowering=False)

    def din(name, shape):
        return nc.dram_tensor(name, shape, F32, kind="ExternalInput").ap()

    xT = din("xT", [D, NT])
    svec = din("svec", [128, NS])
    cst = din("cst", [128, 4 * 128])
    bands = din("bands", [5 * 4 * 144, 128])
    flag_in = din("flag", [128, 1])
    w_in = din("w_in", [L * D, INW])
    wlr = din("wlr", [L * 32, 2 * 512])
    w_grp = din("w_grp", [L * 4 * 128, 128])
    w_ba = din("w_ba", [L * D, D])
    w_bb = din("w_bb", [L * 512, D])
    w_out = din("w_out", [L * D, D])
    w_up = din("w_up", [L * D, 2 * DFF])
    w_dn = din("w_dn", [L * DFF, D])
    Y = nc.dram_tensor("Y", [D, NT], F32, kind="ExternalOutput").ap()

    def dscr(name, shape, dt):
        return nc.dram_tensor(name, shape, dt).ap()

    Xs = dscr("Xs", [D, NT], F32)
    XM = dscr("XM", [D, NT + 2], F32)
    QB = dscr("QB", [512, NT], BF16)
    KB = dscr("KB", [NT, 512], BF16)
    Vs = dscr("Vs", [NT, 1024], BF16)
    Os = dscr("Os", [D, NT], BF16)
    Rs = dscr("Rs", [D, NT], BF16)
    GAs = dscr("GAs", [D, NT], BF16)
    GBs = dscr("GBs", [D, NT], BF16)
    Us = dscr("Us", [NT + 16, 512], BF16)

    P = Prog(nc)
    dbufs = {}

    def DB(name, t):
        k = (name, t)
        if k not in dbufs:
            dbufs[k] = Buf("%s%d" % (name, t))
        return dbufs[k]

    st = ExitStack()
    with st:
        ar = Arena(nc, st, 207 * 1024)
        pbank = [st.enter_context(nc.psum_tensor("pb%d" % i, [128, 512], F32)) for i in range(8)]
        pbuf = [Buf("pb%d" % i) for i in range(8)]

        cbf = ar.alloc([4, 128], BF16)
        ident = cbf[:, 0, :]
        ones = cbf[:, 1, :]
        mask_f = cbf[:, 2, :]
        mask_b = cbf[:, 3, :]
        zer = ar.alloc([128], F32)
        sv = ar.alloc([NS], F32)
        negb = ar.alloc([8 * L], F32)
        flag = ar.alloc([1], F32)
        c_start = ar.off
        S32 = ar.alloc([H, DV], F32)
        Sbf = ar.alloc([H, DV], BF16)
        abfirst = ar.alloc([H, NCK], F32)
        akeep = ar.alloc([H], F32)
        B_akeep = Buf('akeep')
        B_const = Buf("const")
        B_S32 = [Buf("S32_%d" % h) for h in range(H)]
        B_Sbf = [Buf("Sbf_%d" % h) for h in range(H)]
        B_abf = Buf("abfirst")
        persist_end = ar.off

        stg0 = ar.alloc([4 * 128], F32)
        P.dma("c0", stg0, cst, writes=[B_const])
        P.dve(lambda e: e.tensor_copy(cbf.rearrange("p a b -> p (a b)"), stg0), [B_const], [B_const])
        P.dma("c1", sv, svec, writes=[B_const])
        P.dma("c2", flag, flag_in, writes=[B_const])
        P.pool(lambda e: e.memset(zer, 0.0), [], [B_const])
        P.dve(lambda e: e.tensor_scalar(negb, sv[:, SVO["bf"]:SVO["bf"] + 8 * L], -1.0, None, ALU.mult), [B_const], [B_const])
        zb = ar.alloc([8, 512], BF16)
        P.pool(lambda e: e.memset(zb, 0.0), [], [B_const])
        zb3 = zb.bitcast(F32) if False else None
        zf = ar.alloc([8, 1], F32)
        P.pool(lambda e: e.memset(zf, 0.0), [], [B_const])
        P.dma("g0", Us[0:8, :], zb[0:8, 0, :], reads=[B_const], writes=[DB("U", -1)])
        P.dma("g1", Us[NT + 8:NT + 16, :], zb[0:8, 0, :], reads=[B_const], writes=[DB("U", NTF)])
        P.dma("g2", XM[:, 0:1].rearrange("(k p) n -> p k n", p=128), zf, reads=[B_const], writes=[DB("XM", -1)], slow=True)
        P.dma("g3", XM[:, NT + 1:NT + 2].rearrange("(k p) n -> p k n", p=128), zf, reads=[B_const], writes=[DB("XM", NTF)], slow=True)
        P.barrier()
        ar.off = persist_end

        rr = [0]

        def load_w(dst, src, nk, ncols, scale_col=None, cuts=(), muls=None, stg=None, stgB=None):
            edges = sorted(set([0, ncols] + [c for c in cuts if 0 < c < ncols]))
            blocks = []
            for a, b in zip(edges[:-1], edges[1:]):
                c = a
                while c < b:
                    blocks.append((c, min(b, c + 2048)))
                    c += 2048
            for k in range(nk):
                for (c0, c1) in blocks:
                    i = rr[0] % 2
                    rr[0] += 1
                    s = stg[i][:, 0:c1 - c0]
                    P.dma("stg%d" % i, s, src[k * 128:(k + 1) * 128, c0:c1], writes=[stgB[i]])
                    o = dst[:, k, c0:c1]
                    mul = None
                    if muls:
                        for (m0, m1, mv) in muls:
                            if c0 >= m0 and c1 <= m1:
                                mul = mv
                    sc = sv[:, scale_col + k:scale_col + k + 1] if scale_col is not None else None
                    eng = ("act", "dve")[rr[0] % 2]
                    if mul is not None and eng == "act":
                        eng = "dve"
                    if eng == "act":
                        if sc is None:
                            P.act(lambda e, o=o, s=s: e.copy(o, s), [stgB[i]], [])
                        else:
                            P.act(lambda e, o=o, s=s, sc=sc: e.activation(o, s, AF.Identity, scale=sc), [stgB[i], B_const], [])
                    else:
                        if sc is None and mul is None:
                            f = lambda e, o=o, s=s: e.tensor_copy(o, s)
                        elif mul is None:
                            f = lambda e, o=o, s=s, sc=sc: e.tensor_scalar(o, s, sc, None, ALU.mult)
                        elif sc is None:
                            f = lambda e, o=o, s=s, mul=mul: e.tensor_scalar(o, s, float(mul), None, ALU.mult)
                        else:
                            f = lambda e, o=o, s=s, sc=sc, mul=mul: e.tensor_scalar(o, s, sc, float(mul), ALU.mult, ALU.mult)
                        P.add(eng, f, [stgB[i], B_const], [])

        def mm(out, lhsT, rhs, start, stop, reads, writes):
            P.pe(lambda e: e.matmul(out, lhsT, rhs, start=start, stop=stop), reads, writes)

        for l in range(L):
            last_layer = (l == L - 1)
            Xsrc = xT if l == 0 else Xs
            ar.off = persist_end
            WIN = ar.alloc([KC, INW], BF16)
            WLR = ar.alloc([2, 512], BF16, parts=32)
            xt = ar.alloc([KC, 512], F32)
            xn = ar.alloc([KC, 512], BF16)
            lnt = ar.alloc([512], F32)
            rstd = ar.alloc([512], F32)
            ring_off = ar.off
            ring = [ar.alloc([4, 512], BF16) for _ in range(4)]
            stg = [ar.alloc([2048], F32, at=ring_off), ar.alloc([2048], F32, at=ring_off + 8192)]
            sq = ar.alloc([KC, 512], BF16)
            lrT = ar.alloc([512], BF16, parts=32)
            vtm = ar.alloc([4, 1024], BF16)
            utm = ar.alloc([4, 512], BF16)
            kbstg = ar.alloc([4, 512], BF16)
            ost = [ar.alloc([2, 512], BF16) for _ in range(2)]
            qbs = [ar.alloc([512], BF16) for _ in range(2)]
            el = ar.alloc([512], F32)
            bc = ar.alloc([512], F32)
            Asets = [[ar.alloc([512], F32) for _ in range(4)] for _ in range(2)]
            Anames = [["Af%d" % i_, "Aif%d" % i_, "Ab%d" % i_, "Aib%d" % i_] for i_ in range(2)]
            qf = ar.alloc([512], BF16)
            kf = ar.alloc([512], BF16)
            kbT = ar.alloc([512], BF16)
            ktm4 = ar.alloc([4, 128], BF16)
            B_P4 = Buf('P4')
            sTf = [ar.alloc([128], BF16) for _ in range(2)]
            sTb = [ar.alloc([128], BF16) for _ in range(2)]
            B = {n: Buf("F_" + n) for n in ("xt xn lnt rstd sq lrT vtm utm kbstg el bc Af0 Aif0 Ab0 Aib0 Af1 Aif1 Ab1 Aib1 qf kf kbT "
                                          "ring0 ring1 ring2 ring3 ost0 ost1 qbs0 qbs1 ktm4 sTf0 sTf1 sTb0 sTb1 stg0 stg1").split()}
            stgB = [B["stg0"], B["stg1"]]
            load_w(WIN, w_in[l * D:(l + 1) * D, :], KC, INW, scale_col=SVO["g1"] + 8 * l, cuts=(512,),
                   muls=[(0, 512, DK ** -0.5)], stg=stg, stgB=stgB)
            P.dma("stg0", stg[0][0:32, 0:1024], wlr[l * 32:(l + 1) * 32, :], writes=[stgB[0]])
            P.dve(lambda e: e.tensor_copy(WLR.rearrange("p a b -> p (a b)"), stg[0][0:32, 0:1024]), [stgB[0]], [])
            P.barrier()
            pjr = [0]
            ringi = [0]

            def next_pj():
                i = pjr[0] % 3
                pjr[0] += 1
                return i

            for t in range(NTF):
                t0 = t * 512
                unit_start = (t0 % UNIT == 0)
                if t0 == 0:
                    P.dve(lambda e: e.memset(akeep, 0.0), [], [B_akeep])
                    for h in range(H):
                        P.dve(lambda e, h=h: e.memset(S32[:, h, :], 0.0), [], [B_S32[h]])
                        P.dve(lambda e, h=h: e.memset(Sbf[:, h, :], 0.0), [], [B_Sbf[h]])
                elif unit_start:
                    for h in range(H):
                        P.dve(lambda e, h=h: e.tensor_scalar(S32[:, h, :], S32[:, h, :], flag[:, 0:1], None, ALU.mult),
                              [B_S32[h], B_const], [B_S32[h]])
                        P.dve(lambda e, h=h: e.tensor_scalar(Sbf[:, h, :], Sbf[:, h, :], flag[:, 0:1], None, ALU.mult),
                              [B_Sbf[h], B_const], [B_Sbf[h]])
                if t == 0:
                    P.dma("F_xt", xt, Xsrc[:, t0:t0 + 512].rearrange("(k p) n -> p k n", p=128),
                          reads=[DB("X", t)], writes=[B["xt"]])
                P.act(lambda e: e.activation(sq, xt, AF.Square), [B["xt"]], [B["sq"]])
                for k in range(KC):
                    mm(pbank[3][:, :], ones, sq[:, k, :], k == 0, k == KC - 1, [B["sq"], B_const], [pbuf[3]])
                P.act(lambda e: e.activation(lnt, pbank[3][:, :], AF.Ln, scale=1.0 / D, bias=EPS), [pbuf[3]], [B["lnt"]])
                P.act(lambda e: e.activation(rstd, lnt, AF.Exp, scale=-0.5), [B["lnt"]], [B["rstd"]])
                P.dve(lambda e: e.tensor_tensor(xn, xt, rstd.unsqueeze(1).broadcast_to([128, KC, 512]), ALU.mult),
                      [B["xt"], B["rstd"]], [B["xn"]])
                if t + 1 < NTF:
                    P.dma("F_xt", xt, Xsrc[:, t0 + 512:t0 + 1024].rearrange("(k p) n -> p k n", p=128),
                          reads=[DB("X", t + 1)], writes=[B["xt"]])
                pj = next_pj()
                for k in range(KC):
                    mm(pbank[pj][0:32, :], WIN[:, k, LR0:LR0 + 32], xn[:, k, :], k == 0, k == KC - 1, [B["xn"]], [pbuf[pj]])
                P.act(lambda e, pj=pj: e.copy(lrT, pbank[pj][0:32, :]), [pbuf[pj]], [B["lrT"]])
                def rgg_groups(t=t, t0=t0):
                    for (col0, dst, dname) in ((R0, Rs, "R"), (GA0, GAs, "GA"), (GB0, GBs, "GB")):
                        for half in range(2):
                            ri = ringi[0] % 4
                            ringi[0] += 1
                            rb = ring[ri]
                            for j in range(4):
                                c = half * 4 + j
                                pj = next_pj()
                                for k in range(KC):
                                    mm(pbank[pj][:, :], WIN[:, k, col0 + c * 128:col0 + (c + 1) * 128], xn[:, k, :],
                                       k == 0, k == KC - 1, [B["xn"]], [pbuf[pj]])
                                P.act(lambda e, pj=pj, rb=rb, j=j: e.copy(rb[:, j, :], pbank[pj][:, :]),
                                      [pbuf[pj]], [B["ring%d" % ri]])
                                if j == 3:
                                    P.dma("F_ring%d" % ri,
                                          dst[half * 512:(half + 1) * 512, t0:t0 + 512].rearrange("(k p) n -> p k n", p=128), rb,
                                          reads=[B["ring%d" % ri]], writes=[DB(dname, t)])
                                yield
                rgg = rgg_groups()

                def rgg_step(n=1):
                    for _ in range(n):
                        next(rgg, None)
                for tc in range(4):
                    for cg in range(2):
                        pj = next_pj()
                        for k in range(KC):
                            mm(pbank[pj][:, :], xn[:, k, tc * 128:(tc + 1) * 128], WIN[:, k, V0 + cg * 512:V0 + (cg + 1) * 512],
                               k == 0, k == KC - 1, [B["xn"]], [pbuf[pj]])
                        P.dve(lambda e, pj=pj, tc=tc, cg=cg: e.tensor_copy(vtm[:, tc, cg * 512:(cg + 1) * 512], pbank[pj][:, :]),
                              [pbuf[pj]], [B["vtm"]])
                P.dma("F_vtm", Vs[t0:t0 + 512, :].rearrange("(c p) n -> p c n", p=128), vtm,
                      reads=[B["vtm"]], writes=[DB("V", t)])
                for tc in range(4):
                    pj = next_pj()
                    for k in range(KC):
                        mm(pbank[pj][:, :], xn[:, k, tc * 128:(tc + 1) * 128], WIN[:, k, U0:U0 + 512],
                           k == 0, k == KC - 1, [B["xn"]], [pbuf[pj]])
                    P.dve(lambda e, pj=pj, tc=tc: e.tensor_copy(utm[:, tc, :], pbank[pj][:, :]), [pbuf[pj]], [B["utm"]])
                P.dma("F_utm", Us[8 + t0:8 + t0 + 512, :].rearrange("(c p) n -> p c n", p=128), utm,
                      reads=[B["utm"]], writes=[DB("U", t)])
                def decays(h, A4, A4n):
                    for dr in range(2):
                        A_, Ai_ = A4[2 * dr], A4[2 * dr + 1]
                        An, Ain = A4n[2 * dr], A4n[2 * dr + 1]
                        mm(pbank[3][:, :], WLR[0:32, dr, h * 128:(h + 1) * 128], lrT[0:32, :], True, True, [B["lrT"]], [pbuf[3]])
                        nb_ = negb[:, (dr * L + l) * 4 + h:(dr * L + l) * 4 + h + 1]
                        P.act(lambda e, nb_=nb_: e.activation(el, pbank[3][:, :], AF.Exp, scale=-1.0, bias=nb_),
                              [pbuf[3], B_const], [B["el"]])
                        P.act(lambda e: e.activation(el, el, AF.Ln, bias=1.0), [B["el"]], [B["el"]])
                        for c in range(4):
                            if dr == 0:
                                P.dve(lambda e, c=c: e.tensor_tensor_scan(bc[:, c * 128:(c + 1) * 128], el[:, c * 128:(c + 1) * 128],
                                                                          zer, 0.0, ALU.add, ALU.add),
                                      [B["el"], B_const], [B["bc"]])
                            else:
                                P.dve(lambda e, c=c: e.tensor_tensor_scan(rev2(bc[:, c * 128:(c + 1) * 128]),
                                                                          rev2(el[:, c * 128:(c + 1) * 128]),
                                                                          zer, 0.0, ALU.add, ALU.add),
                                      [B["el"], B_const], [B["bc"]])
                        P.act(lambda e, A_=A_: e.activation(A_, bc, AF.Exp, scale=-1.0 / 16), [B["bc"]], [B[An]])
                        P.act(lambda e, Ai_=Ai_: e.activation(Ai_, bc, AF.Exp, scale=1.0 / 16), [B["bc"]], [B[Ain]])

                decays(0, Asets[0], Anames[0])
                for h in range(H):
                    Af, Aif, Ab, Aib = Asets[h % 2]
                    nAf, nAif, nAb, nAib = Anames[h % 2]
                    pq = next_pj()
                    for k in range(KC):
                        mm(pbank[pq][:, :], WIN[:, k, Q0 + h * 128:Q0 + (h + 1) * 128], xn[:, k, :], k == 0, k == KC - 1,
                           [B["xn"]], [pbuf[pq]])
                    pk = next_pj()
                    for k in range(KC):
                        mm(pbank[pk][:, :], WIN[:, k, K0 + h * 128:K0 + (h + 1) * 128], xn[:, k, :], k == 0, k == KC - 1,
                           [B["xn"]], [pbuf[pk]])
                    qi = h % 2
                    qb_ = qbs[qi]
                    P.dve(lambda e, pq=pq, Af=Af: e.tensor_tensor(qf, pbank[pq][:, :], Af, ALU.mult), [pbuf[pq], B[nAf]], [B["qf"]])
                    P.dve(lambda e, pq=pq, qb_=qb_, Ab=Ab: e.tensor_tensor(qb_, pbank[pq][:, :], Ab, ALU.mult),
                          [pbuf[pq], B[nAb]], [B["qbs%d" % qi]])
                    P.dve(lambda e, pk=pk, Aif=Aif: e.tensor_tensor(kf, pbank[pk][:, :], Aif, ALU.mult), [pbuf[pk], B[nAif]], [B["kf"]])
                    P.dve(lambda e, pk=pk, Aib=Aib: e.tensor_tensor(kbT, pbank[pk][:, :], Aib, ALU.mult), [pbuf[pk], B[nAib]], [B["kbT"]])
                    P.dma("F_qbs%d" % qi, QB[h * 128:(h + 1) * 128, t0:t0 + 512], qb_, reads=[B["qbs%d" % qi]], writes=[DB("QB", t)])
                    P.pool(lambda e, h=h, t=t, Ab=Ab: e.tensor_copy(abfirst[:, h, t * 4:(t + 1) * 4],
                                                              Ab.rearrange("p (c j) -> p c j", j=128)[:, :, 0]),
                           [B[nAb]], [B_abf])
                    oi = h % 2
                    trv = pbank[7].bitcast(BF16)
                    for c in range(4):
                        cs = slice(c * 128, (c + 1) * 128)
                        P.pe(lambda e, c=c, cs=cs: e.transpose(trv[:, c * 128:(c + 1) * 128], kf[:, cs], ident), [B["kf"], B_const], [pbuf[7]])
                        P.pe(lambda e, c=c, cs=cs: e.transpose(trv[:, 512 + c * 128:512 + (c + 1) * 128], kbT[:, cs], ident), [B["kbT"], B_const], [pbuf[7]])
                    P.act(lambda e: e.copy(ktm4, trv[:, 0:512].rearrange("p (c j) -> p c j", j=128)), [pbuf[7]], [B["ktm4"]])
                    P.act(lambda e, h=h: e.copy(kbstg[:, :, h * 128:(h + 1) * 128], trv[:, 512:1024].rearrange("p (c j) -> p c j", j=128)),
                          [pbuf[7]], [B["kbstg"]])
                    if h + 1 < H:
                        decays(h + 1, Asets[(h + 1) % 2], Anames[(h + 1) % 2])
                    for c in range(4):
                        cs = slice(c * 128, (c + 1) * 128)
                        si = c % 2
                        scb_ = pbuf[4]
                        mm(pbank[4][:, 0:128], kf[:, cs], qf[:, cs], True, True, [B["kf"], B["qf"]], [scb_])
                        mm(pbank[4][:, 128:256], kbT[:, cs], qb_[:, cs], True, True,
                           [B["kbT"], B["qbs%d" % qi]], [scb_])
                        P.dve(lambda e, si=si: e.tensor_tensor(sTf[si], pbank[4][:, 0:128], mask_f, ALU.mult),
                              [scb_, B_const], [B["sTf%d" % si]])
                        P.dve(lambda e, si=si: e.tensor_tensor(sTb[si], pbank[4][:, 128:256], mask_b, ALU.mult),
                              [scb_, B_const], [B["sTb%d" % si]])
                        rgg_step(1)
                        for vc in range(2):
                            vcol = h * 256 + vc * 128
                            ob = 5 + vc
                            mm(pbank[ob][:, cs], vtm[:, c, vcol:vcol + 128], sTf[si], True, False,
                               [B["vtm"], B["sTf%d" % si]], [pbuf[ob]])
                            mm(pbank[ob][:, cs], vtm[:, c, vcol:vcol + 128], sTb[si], False, False,
                               [B["vtm"], B["sTb%d" % si]], [pbuf[ob]])
                            mm(pbank[ob][:, cs], Sbf[:, h, vc * 128:(vc + 1) * 128], qf[:, cs], False, True,
                               [B_Sbf[h], B["qf"]], [pbuf[ob]])
                        mm(pbank[4][:, 256:512], ktm4[:, c, :], vtm[:, c, h * 256:(h + 1) * 256], True, True,
                           [B["ktm4"], B["vtm"]], [B_P4])
                        aprev = akeep[:, h:h + 1] if c == 0 else Af[:, (c - 1) * 128 + 127:(c - 1) * 128 + 128]
                        P.dve(lambda e, h=h, aprev=aprev: e.scalar_tensor_tensor(S32[:, h, :], S32[:, h, :], aprev, pbank[4][:, 256:512],
                                                                                 ALU.mult, ALU.add),
                              [B_P4, B_S32[h], B[nAf], B_akeep], [B_S32[h]])
                        alast = Af[:, c * 128 + 127:c * 128 + 128]
                        P.dve(lambda e, h=h, alast=alast: e.tensor_scalar(Sbf[:, h, :], S32[:, h, :], alast, None, ALU.mult),
                              [B_S32[h], B[nAf]], [B_Sbf[h]])
                        if c == 3:
                            P.dve(lambda e, h=h, alast=alast: e.tensor_copy(akeep[:, h:h + 1], alast), [B[nAf]], [B_akeep])
                        if c % 2 == 1:
                            rgg_step(1)
                    os_ = ost[oi]
                    P.act(lambda e, os_=os_: e.copy(os_[:, 0, :], pbank[5][:, :]), [pbuf[5]], [B["ost%d" % oi]])
                    P.dve(lambda e, os_=os_: e.tensor_copy(os_[:, 1, :], pbank[6][:, :]), [pbuf[6]], [B["ost%d" % oi]])
                    P.dma("F_ost%d" % oi, Os[h * 256:(h + 1) * 256, t0:t0 + 512].rearrange("(k p) n -> p k n", p=128), os_,
                          reads=[B["ost%d" % oi]], writes=[DB("O", t)])
                rgg_step(24)
                P.dma("F_kbstg", KB[t0:t0 + 512, :].rearrange("(c p) n -> p c n", p=128), kbstg,
                      reads=[B["kbstg"]], writes=[DB("KB", t)])
            P.barrier()

            ar.off = persist_end
            WBA = ar.alloc([KC, D], BF16)
            WBB = ar.alloc([4, D], BF16)
            WOUT = ar.alloc([KC, D], BF16)
            WGRP = ar.alloc([4, 128], BF16)
            bandc = ar.alloc([20, 128], BF16)
            bandh = ar.alloc([20, 128], BF16, parts=16)
            Sb32 = ar.alloc([H, DV], F32)
            Sbb = ar.alloc([H, DV], BF16)
            rt = ar.alloc([KC, 512], BF16)
            gat = ar.alloc([KC, 512], BF16)
            gbt = ar.alloc([KC, 512], BF16)
            sets = []
            for si_ in range(2):
                o_ = ar.off
                ot_ = ar.alloc([KC, 512], BF16)
                m2_ = ar.alloc([KC, 512], BF16, at=o_)
                o_ = ar.off
                qbt_ = ar.alloc([H, 512], BF16)
                pbf_ = ar.alloc([4, 512], BF16, at=o_)
                o_ = ar.off
                kbt_ = ar.alloc([4, 512], BF16)
                p2bf_ = ar.alloc([4, 512], BF16, at=o_)
                o_ = ar.off
                vt_ = ar.alloc([4, 1024], BF16)
                m1_ = ar.alloc([KC, 512], BF16, at=o_)
                sets.append(dict(ot=ot_, m2=m2_, qbt=qbt_, pbf=pbf_, kbt=kbt_, p2bf=p2bf_, vt=vt_, m1=m1_,
                                 Bot=Buf("B_ot%d" % si_), Bqbt=Buf("B_qbt%d" % si_), Bkbt=Buf("B_kbt%d" % si_), Bvt=Buf("B_vt%d" % si_)))
            ut = ar.alloc([4, 512], BF16)
            uh = ar.alloc([4, 512], BF16, parts=16)
            xtb = ar.alloc([KC, 512], F32)
            stg_off = ar.off
            osum = ar.alloc([KC, 512], F32)
            stgb = [ar.alloc([2048], F32, at=stg_off), ar.alloc([2048], F32, at=stg_off + 8192)]
            sqy = ar.alloc([KC, 512], BF16)
            rstd4 = ar.alloc([H, 512], F32)
            lnb = ar.alloc([512], F32)
            B = {n: Buf("B_" + n) for n in ("rt gat gbt ut uh xtb osum sqy rstd4 lnb stg0 stg1").split()}
            B_Sb32 = [Buf("Sb32_%d" % h) for h in range(H)]
            B_Sbb = [Buf("Sbb_%d" % h) for h in range(H)]
            stgB = [B["stg0"], B["stg1"]]
            load_w(WBA, w_ba[l * D:(l + 1) * D, :], KC, D, scale_col=SVO["og"] + 8 * l, stg=stgb, stgB=stgB)
            load_w(WBB, w_bb[l * 512:(l + 1) * 512, :], 4, D, scale_col=SVO["ps"] + 4 * l, stg=stgb, stgB=stgB)
            load_w(WOUT, w_out[l * D:(l + 1) * D, :], KC, D, stg=stgb, stgB=stgB)
            load_w(WGRP, w_grp[l * 512:(l + 1) * 512, :], 4, 128, stg=stgb, stgB=stgB)
            bsrc = bands.rearrange("(m r) n -> r m n", r=144)
            for half in range(2):
                i = rr[0] % 2
                rr[0] += 1
                s3 = stgb[i][:, 0:1280].rearrange("p (m n) -> p m n", n=128)
                P.dma("stg%d" % i, s3, bsrc[0:128, half * 10:(half + 1) * 10, :], writes=[stgB[i]])
                P.dve(lambda e, s3=s3, half=half: e.tensor_copy(bandc[:, half * 10:(half + 1) * 10, :], s3), [stgB[i]], [])
                i = rr[0] % 2
                rr[0] += 1
                s3h = stgb[i][0:16, 0:1280].rearrange("p (m n) -> p m n", n=128)
                P.dma("stg%d" % i, s3h, bsrc[128:144, half * 10:(half + 1) * 10, :], writes=[stgB[i]])
                P.dve(lambda e, s3h=s3h, half=half: e.tensor_copy(bandh[:, half * 10:(half + 1) * 10, :], s3h), [stgB[i]], [])
            P.barrier()
            pjr = [0]

            def next_pj():
                i = pjr[0] % 3
                pjr[0] += 1
                return i

            def load_set(tt, S_):
                tt0 = tt * 512
                P.dma("B_qbt%d" % S_["i"], S_["qbt"], QB[:, tt0:tt0 + 512].rearrange("(k p) n -> p k n", p=128), reads=[DB("QB", tt)], writes=[S_["Bqbt"]])
                P.dma("B_kbt%d" % S_["i"], S_["kbt"], KB[tt0:tt0 + 512, :].rearrange("(c p) n -> p c n", p=128), reads=[DB("KB", tt)], writes=[S_["Bkbt"]])
                P.dma("B_vt%d" % S_["i"], S_["vt"], Vs[tt0:tt0 + 512, :].rearrange("(c p) n -> p c n", p=128), reads=[DB("V", tt)], writes=[S_["Bvt"]])
                P.dma("B_ot%d" % S_["i"], S_["ot"], Os[:, tt0:tt0 + 512].rearrange("(k p) n -> p k n", p=128), reads=[DB("O", tt)], writes=[S_["Bot"]])

            sets[0]["i"] = 0
            sets[1]["i"] = 1
            load_set(NTF - 1, sets[0])
            for t in range(NTF - 1, -1, -1):
                t0 = t * 512
                S_ = sets[(NTF - 1 - t) % 2]
                ot, m2, qbt, pbf, kbt, p2bf, vt, m1 = (S_[k_] for k_ in ("ot", "m2", "qbt", "pbf", "kbt", "p2bf", "vt", "m1"))
                B["ot"] = B["m2"] = S_["Bot"]
                B["qbt"] = B["pbf"] = S_["Bqbt"]
                B["kbt"] = B["p2bf"] = S_["Bkbt"]
                B["vt"] = B["m1"] = S_["Bvt"]
                if t > 0:
                    load_set(t - 1, sets[(NTF - t) % 2])
                if t == NTF - 1:
                    for h in range(H):
                        P.dve(lambda e, h=h: e.memset(Sb32[:, h, :], 0.0), [], [B_Sb32[h]])
                        P.dve(lambda e, h=h: e.memset(Sbb[:, h, :], 0.0), [], [B_Sbb[h]])
                elif (t0 + 512) % UNIT == 0:
                    for h in range(H):
                        P.dve(lambda e, h=h: e.tensor_scalar(Sb32[:, h, :], Sb32[:, h, :], flag[:, 0:1], None, ALU.mult),
                              [B_Sb32[h], B_const], [B_Sb32[h]])
                        P.dve(lambda e, h=h: e.tensor_scalar(Sbb[:, h, :], Sbb[:, h, :], flag[:, 0:1], None, ALU.mult),
                              [B_Sbb[h], B_const], [B_Sbb[h]])
                fm = lambda T_: T_[:, t0:t0 + 512].rearrange("(k p) n -> p k n", p=128)
                P.dma("B_rt", rt, fm(Rs), reads=[DB("R", t)], writes=[B["rt"]])
                P.dma("B_gat", gat, fm(GAs), reads=[DB("GA", t)], writes=[B["gat"]])
                P.dma("B_gbt", gbt, fm(GBs), reads=[DB("GB", t)], writes=[B["gbt"]])
                P.dma("B_ut", ut, Us[8 + t0:8 + t0 + 512, :].rearrange("(c p) n -> p c n", p=128), reads=[DB("U", t)], writes=[B["ut"]])
                P.dma("B_uh", uh[0:8], Us[t0:t0 + 512, :].rearrange("(c p) n -> p c n", p=128)[0:8],
                      reads=[DB("U", t - 1), DB("U", t)], writes=[B["uh"]])
                uh2src = AP(Us.tensor, Us.offset + (8 + t0 + 128) * 512, [[512, 8], [128 * 512, 4], [1, 512]])
                P.dma("B_uh2", uh[8:16], uh2src,
                      reads=[DB("U", t + 1), DB("U", t)], writes=[B["uh"]])
                P.dma("B_xtb", xtb, fm(Xsrc), reads=[DB("X", t)], writes=[B["xtb"]])
                P.act(lambda e: e.activation(sqy, rt, AF.Sigmoid), [B["rt"]], [B["sqy"]])
                P.pool(lambda e: e.tensor_tensor(rt, rt, sqy, ALU.mult), [B["rt"], B["sqy"]], [B["rt"]])
                P.act(lambda e: e.activation(gat, gat, AF.Sigmoid), [B["gat"]], [B["gat"]])
                P.act(lambda e: e.activation(gbt, gbt, AF.Sigmoid), [B["gbt"]], [B["gbt"]])
                osum4 = osum.rearrange("p (h v) n -> p h v n", v=2)
                ot4 = ot.rearrange("p (h v) n -> p h v n", v=2)
                for c in range(3, -1, -1):
                    cs = slice(c * 128, (c + 1) * 128)
                    obk = (5, 6) if c % 2 == 1 else (0, 1)
                    gck = t * 4 + c
                    gprev = min(gck + 1, NCK - 1)
                    for h in range(H):
                        for vc in range(2):
                            mm(pbank[obk[vc]][:, h * 128:(h + 1) * 128], Sbb[:, h, vc * 128:(vc + 1) * 128], qbt[:, h, cs], True, True,
                               [B_Sbb[h], B["qbt"]], [pbuf[obk[vc]]])
                        pb_, pc_ = (7, (h % 2) * 256) if h < 2 else (4, (h % 2) * 256)
                        mm(pbank[pb_][:, pc_:pc_ + 256], kbt[:, c, h * 128:(h + 1) * 128], vt[:, c, h * 256:(h + 1) * 256], True, True,
                           [B["kbt"], B["vt"]], [pbuf[pb_]])
                        aprev = abfirst[:, h, gprev:gprev + 1]
                        P.dve(lambda e, h=h, aprev=aprev, pb_=pb_, pc_=pc_: e.scalar_tensor_tensor(
                            Sb32[:, h, :], Sb32[:, h, :], aprev, pbank[pb_][:, pc_:pc_ + 256], ALU.mult, ALU.add),
                              [pbuf[pb_], B_Sb32[h], B_abf], [B_Sb32[h]])
                        afirst = abfirst[:, h, gck:gck + 1]
                        P.dve(lambda e, h=h, afirst=afirst: e.tensor_scalar(Sbb[:, h, :], Sb32[:, h, :], afirst, None, ALU.mult),
                              [B_Sb32[h], B_abf], [B_Sbb[h]])
                    for vc in range(2):
                        P.dve(lambda e, vc=vc, cs=cs, obk=obk, ot4=ot4: e.tensor_tensor(
                            osum4[:, :, vc, cs], pbank[obk[vc]][:, :].rearrange("p (h j) -> p h j", j=128), ot4[:, :, vc, cs], ALU.add),
                              [pbuf[obk[vc]], B["ot"]], [B["osum"]])
                def pool_branch():
                    for g in range(4):
                        for c in range(4):
                            gck = t * 4 + c
                            pos = gck % (UNIT // 128)
                            if pos == 0:
                                cls = 0 if gck == 0 else 2
                            elif pos == UNIT // 128 - 1:
                                cls = 1 if gck < UNIT // 128 else 3
                            else:
                                cls = 4
                            cs = slice(c * 128, (c + 1) * 128)
                            mm(pbank[4][:, cs], ut[:, c, g * 128:(g + 1) * 128], bandc[:, cls * 4 + g, :], True, False, [B["ut"]], [pbuf[4]])
                            mm(pbank[4][:, cs], uh[0:16, c, g * 128:(g + 1) * 128], bandh[0:16, cls * 4 + g, :], False, True, [B["uh"]], [pbuf[4]])
                        P.act(lambda e, g=g, pbf=pbf: e.copy(pbf[:, g, :], pbank[4][:, :]), [pbuf[4]], [B["pbf"]])
                        pj = next_pj()
                        mm(pbank[pj][:, :], WGRP[:, g, :], pbf[:, g, :], True, True, [B["pbf"]], [pbuf[pj]])
                        P.act(lambda e, g=g, pj=pj, p2bf=p2bf: e.copy(p2bf[:, g, :], pbank[pj][:, :]), [pbuf[pj]], [B["p2bf"]])
                    for oc in range(KC):
                        pj = next_pj()
                        for g in range(4):
                            mm(pbank[pj][:, :], WBB[:, g, oc * 128:(oc + 1) * 128], p2bf[:, g, :], g == 0, g == 3, [B["p2bf"]], [pbuf[pj]])
                        P.dve(lambda e, pj=pj, oc=oc, m2=m2: e.tensor_tensor(m2[:, oc, :], pbank[pj][:, :], gbt[:, oc, :], ALU.mult),
                              [pbuf[pj], B["gbt"]], [B["m2"]])
                P.act(lambda e: e.activation(sqy, osum, AF.Square), [B["osum"]], [B["sqy"]])
                pool_branch()
                for h in range(H):
                    for vc in range(2):
                        mm(pbank[3][:, :], ones, sqy[:, h * 2 + vc, :], vc == 0, vc == 1, [B["sqy"], B_const], [pbuf[3]])
                    P.act(lambda e: e.activation(lnb, pbank[3][:, :], AF.Ln, scale=1.0 / DV, bias=EPS), [pbuf[3]], [B["lnb"]])
                    P.act(lambda e, h=h: e.activation(rstd4[:, h, :], lnb, AF.Exp, scale=-0.5), [B["lnb"]], [B["rstd4"]])
                P.dve(lambda e: e.tensor_tensor(osum4, osum4, rstd4.unsqueeze(2).broadcast_to([128, H, 2, 512]), ALU.mult),
                      [B["osum"], B["rstd4"]], [B["osum"]])
                P.dve(lambda e: e.tensor_tensor(sqy, osum, rt, ALU.mult), [B["osum"], B["rt"], B["sqy"]], [B["sqy"]])
                for oc in range(KC):
                    pj = next_pj()
                    for k in range(KC):
                        mm(pbank[pj][:, :], WBA[:, k, oc * 128:(oc + 1) * 128], sqy[:, k, :], k == 0, k == KC - 1, [B["sqy"]], [pbuf[pj]])
                    P.dve(lambda e, pj=pj, oc=oc, m1=m1: e.tensor_tensor(m1[:, oc, :], pbank[pj][:, :], gat[:, oc, :], ALU.mult),
                          [pbuf[pj], B["gat"]], [B["m1"]])
                P.dve(lambda e, m1=m1, m2=m2: e.tensor_tensor(m1, m1, m2, ALU.add), [B["m1"], B["m2"]], [B["m1"]])
                for oc in range(KC):
                    pj = next_pj()
                    for k in range(KC):
                        mm(pbank[pj][:, :], WOUT[:, k, oc * 128:(oc + 1) * 128], m1[:, k, :], k == 0, k == KC - 1, [B["m1"]], [pbuf[pj]])
                    P.dve(lambda e, pj=pj, oc=oc: e.tensor_tensor(xtb[:, oc, :], pbank[pj][:, :], xtb[:, oc, :], ALU.add),
                          [pbuf[pj], B["xtb"]], [B["xtb"]])
                P.dma("B_xtb_st", XM[:, 1 + t0:1 + t0 + 512].rearrange("(k p) n -> p k n", p=128), xtb,
                      reads=[B["xtb"]], writes=[DB("XM", t)])
            P.barrier()

            ar.off = c_start
            WUP = ar.alloc([KC, 2 * DFF], BF16)
            WDN = ar.alloc([FC, D], BF16)
            WM = 510
            WM2 = WM + 2
            xm = ar.alloc([KC, WM2], F32)
            r16_off = ar.off
            xn2 = ar.alloc([KC, WM2], BF16)
            lnc = ar.alloc([WM2], F32)
            rsc = ar.alloc([WM2], F32)
            yc = [ar.alloc([WM2], F32) for _ in range(2)]
            assert ar.off - r16_off == 16384
            xres = ar.alloc([KC, WM2], F32, at=r16_off)
            sc_ = [ar.alloc([WM], BF16) for _ in range(2)]
            sqc = ar.alloc([KC, WM2], BF16)
            stg_off = ar.off
            h2 = ar.alloc([FC, WM2], BF16)
            stgc = [ar.alloc([2048], F32, at=stg_off), ar.alloc([2048], F32, at=stg_off + 8192)]
            assert stg_off + 16384 <= ar.off
            fl32 = ar.alloc([WM2], F32, at=stg_off)
            fr32 = ar.alloc([WM2], F32, at=stg_off + 2048)
            B = {n: Buf("C_" + n) for n in ("xm xn2 lnc rsc yc0 yc1 sc0 sc1 h2 sqc stg0 stg1").split()}
            R16B = [B["xn2"], B["lnc"], B["rsc"], B["yc0"], B["yc1"]]
            stgB = [B["stg0"], B["stg1"]]
            load_w(WUP, w_up[l * D:(l + 1) * D, :], KC, 2 * DFF, scale_col=SVO["g2"] + 8 * l, stg=stgc, stgB=stgB)
            load_w(WDN, w_dn[l * DFF:(l + 1) * DFF, :], FC, D, stg=stgc, stgB=stgB)
            P.barrier()
            pjr = [0]
            CR = [0, 1, 2, 4, 5, 6, 7]
            Xdst = Y if last_layer else Xs
            ctiles = []
            for u in range(2):
                c = 0
                while c < UNIT:
                    w = min(WM, UNIT - c)
                    ctiles.append((u * UNIT + c, w))
                    c += w

            def xm_reads(c0, W):
                ta, tb = max(c0 - 1, 0) // 512, min(c0 + W, NT - 1) // 512
                rd = [DB("XM", i) for i in range(ta, tb + 1)]
                if c0 == 0:
                    rd.append(DB("XM", -1))
                if c0 + W == NT:
                    rd.append(DB("XM", NTF))
                return rd

            def load_xm(c0, W):
                P.dma("C_xm", xm[:, :, 0:W + 2], XM[:, c0:c0 + W + 2].rearrange("(k p) n -> p k n", p=128),
                      reads=xm_reads(c0, W), writes=[B["xm"]])

            lnc2 = sqc.rearrange("p k n -> p (k n)")[:, 0:2 * WM2].bitcast(F32)[:, 0:WM2]
            rsc2 = sqc.rearrange("p k n -> p (k n)")[:, 2 * WM2:4 * WM2].bitcast(F32)[:, 0:WM2]

            def norm_part1(W2):
                P.act(lambda e, W2=W2: e.activation(sqc[:, :, 0:W2], xm[:, :, 0:W2], AF.Square), [B["xm"]], [B["sqc"]])
                for k in range(KC):
                    mm(pbank[3][:, 0:W2], ones, sqc[:, k, 0:W2], k == 0, k == KC - 1, [B["sqc"], B_const], [pbuf[3]])
                P.act(lambda e, W2=W2: e.activation(lnc2[:, 0:W2], pbank[3][:, 0:W2], AF.Ln, scale=1.0 / D, bias=EPS), [pbuf[3]], [B["sqc"]])
                P.act(lambda e, W2=W2: e.activation(rsc2[:, 0:W2], lnc2[:, 0:W2], AF.Exp, scale=-0.5), [B["sqc"]], [B["sqc"]])

            def norm_part2(ti_, W2):
                P.dve(lambda e, W2=W2: e.tensor_tensor(xn2[:, :, 0:W2], xm[:, :, 0:W2],
                                                        rsc2[:, 0:W2].unsqueeze(1).broadcast_to([128, KC, W2]), ALU.mult),
                      [B["xm"], B["sqc"]], [B["xn2"]])
                if ti_ + 1 < len(ctiles):
                    load_xm(*ctiles[ti_ + 1])

            hoist = not last_layer
            load_xm(*ctiles[0])
            for ti, (c0, W) in enumerate(ctiles):
                W2 = W + 2
                if ti == 0 or not hoist:
                    norm_part1(W2)
                    norm_part2(ti, W2)
                left_flag = (c0 == UNIT)
                right_flag = (c0 + W == UNIT)
                pend_h2 = None
                for f in range(FC):
                    ai = f % 2
                    pa = CR[pjr[0] % 7]
                    pjr[0] += 1
                    for k in range(KC):
                        mm(pbank[pa][:, 0:W2], WUP[:, k, f * 128:(f + 1) * 128], xn2[:, k, 0:W2], k == 0, k == KC - 1, [B["xn2"]], [pbuf[pa]])
                    pv = CR[pjr[0] % 7]
                    pjr[0] += 1
                    for k in range(KC):
                        mm(pbank[pv][:, 0:W], WUP[:, k, DFF + f * 128:DFF + (f + 1) * 128], xn2[:, k, 1:1 + W], k == 0, k == KC - 1,
                           [B["xn2"]], [pbuf[pv]])
                    cwo = SVO["cw"] + l * 3 * FC
                    w0 = sv[:, cwo + f:cwo + f + 1]
                    w1 = sv[:, cwo + FC + f:cwo + FC + f + 1]
                    w2 = sv[:, cwo + 2 * FC + f:cwo + 2 * FC + f + 1]
                    cbv = sv[:, SVO["cb"] + l * FC + f:SVO["cb"] + l * FC + f + 1]
                    if left_flag:
                        P.dve(lambda e, pa=pa: e.tensor_scalar(pbank[pa][:, 0:1], pbank[pa][:, 0:1], flag[:, 0:1], None, ALU.mult),
                              [pbuf[pa], B_const], [pbuf[pa]])
                    if right_flag:
                        P.dve(lambda e, pa=pa, W2=W2: e.tensor_scalar(pbank[pa][:, W2 - 1:W2], pbank[pa][:, W2 - 1:W2], flag[:, 0:1], None, ALU.mult),
                              [pbuf[pa], B_const], [pbuf[pa]])
                    P.dve(lambda e, pa=pa, ai=ai, W=W, w0=w0, cbv=cbv: e.tensor_scalar(yc[ai][:, 0:W], pbank[pa][:, 0:W], w0, cbv, ALU.mult, ALU.add),
                          [pbuf[pa], B_const], [B["yc%d" % ai]])
                    P.dve(lambda e, pa=pa, ai=ai, W=W, w1=w1: e.scalar_tensor_tensor(yc[ai][:, 0:W], pbank[pa][:, 1:1 + W], w1, yc[ai][:, 0:W], ALU.mult, ALU.add),
                          [pbuf[pa], B["yc%d" % ai], B_const], [B["yc%d" % ai]])
                    P.dve(lambda e, pa=pa, ai=ai, W=W, w2=w2: e.scalar_tensor_tensor(yc[ai][:, 0:W], pbank[pa][:, 2:2 + W], w2, yc[ai][:, 0:W], ALU.mult, ALU.add),
                          [pbuf[pa], B["yc%d" % ai], B_const], [B["yc%d" % ai]])
                    P.act(lambda e, ai=ai, W=W: e.activation(sc_[ai][:, 0:W], yc[ai][:, 0:W], AF.Silu), [B["yc%d" % ai]], [B["sc%d" % ai]])
                    if pend_h2 is not None:
                        pend_h2()
                    pend_h2 = (lambda ai=ai, pv=pv, f=f, W=W: P.dve(
                        lambda e: e.tensor_tensor(h2[:, f, 0:W], pbank[pv][:, 0:W], sc_[ai][:, 0:W], ALU.mult),
                        [pbuf[pv], B["sc%d" % ai]], [B["h2"]]))
                pend_h2()
                P.dma("C_xres", xres[:, :, 0:W], XM[:, c0 + 1:c0 + 1 + W].rearrange("(k p) n -> p k n", p=128),
                      reads=xm_reads(c0, W), writes=R16B)
                if hoist and ti + 1 < len(ctiles):
                    norm_part1(ctiles[ti + 1][1] + 2)
                for oc in range(KC):
                    pj = CR[pjr[0] % 7]
                    pjr[0] += 1
                    for f in range(FC):
                        mm(pbank[pj][:, 0:W], WDN[:, f, oc * 128:(oc + 1) * 128], h2[:, f, 0:W], f == 0, f == FC - 1, [B["h2"]], [pbuf[pj]])
                    rb_ = R16B[min(max(oc - 3, 0), 4)] if not last_layer else None
                    P.dve(lambda e, pj=pj, oc=oc, W=W: e.tensor_tensor(xres[:, oc, 0:W], pbank[pj][:, 0:W], xres[:, oc, 0:W], ALU.add),
                          [pbuf[pj]] + (R16B if last_layer else [rb_]), R16B if last_layer else [rb_])
                    if not last_layer:
                        P.dma("C_xo%d" % oc, Xdst[oc * 128:(oc + 1) * 128, c0:c0 + W], xres[:, oc, 0:W], reads=[rb_],
                              writes=[DB("X", i) for i in range(c0 // 512, (c0 + W - 1) // 512 + 1)])
                        if oc == 3 and ti + 1 < len(ctiles):
                            norm_part2(ti + 1, ctiles[ti + 1][1] + 2)
                if last_layer:
                    P.act(lambda e, W=W: e.activation(sqc[:, :, 0:W], xres[:, :, 0:W], AF.Square), R16B, [B["sqc"]])
                    for k in range(KC):
                        mm(pbank[3][:, 0:W], ones, sqc[:, k, 0:W], k == 0, k == KC - 1, [B["sqc"], B_const], [pbuf[3]])
                    P.act(lambda e, W=W: e.activation(fl32[:, 0:W], pbank[3][:, 0:W], AF.Ln, scale=1.0 / D, bias=EPS),
                          [pbuf[3]], [B["h2"]])
                    P.act(lambda e, W=W: e.activation(fr32[:, 0:W], fl32[:, 0:W], AF.Exp, scale=-0.5), [B["h2"]], [B["h2"]])
                    for k in range(KC):
                        fgk = sv[:, SVO["fg"] + k:SVO["fg"] + k + 1]
                        P.dve(lambda e, k=k, fgk=fgk, W=W: e.scalar_tensor_tensor(xres[:, k, 0:W], xres[:, k, 0:W], fgk, fr32[:, 0:W], ALU.mult, ALU.mult),
                              R16B + [B["h2"], B_const], R16B)
                if last_layer:
                    wr = [DB("Y", i) for i in range(c0 // 512, (c0 + W - 1) // 512 + 1)]
                    P.dma("C_xo", Xdst[:, c0:c0 + W].rearrange("(k p) n -> p k n", p=128), xres[:, :, 0:W], reads=R16B, writes=wr)
            P.barrier()
        P.add("sp", None, reads=[DB("Y", t) for t in range(NTF)])
        stats = P.emit(st)
    return nc, stats


def host_inputs(inp, cfg, x_units, flags):
    L = cfg["DEPTH"]
    UNIT = cfg["UNIT"]
    svec = small_layout(inp, L)
    eye = np.eye(128, dtype=np.float32)
    jj, ii = np.meshgrid(np.arange(128), np.arange(128), indexing="ij")
    cst = np.concatenate([eye, np.ones((128, 128), np.float32), (jj <= ii).astype(np.float32), (jj >= ii).astype(np.float32)], axis=1)
    bE, bF, bL = band_matrices("E"), band_matrices("first"), band_matrices("last")
    wlr = np.zeros((L, 32, 2, 512), np.float32)
    wlr[:, 0:16, 0, :] = inp["w_lr2_f"]
    wlr[:, 16:32, 1, :] = inp["w_lr2_b"]
    shared = dict(
        svec=svec, cst=np.ascontiguousarray(cst),
        w_in=np.ascontiguousarray(inp["w_in"]).reshape(L * D, INW),
        wlr=wlr.reshape(L * 32, 1024),
        w_grp=np.ascontiguousarray(inp["w_pool_grp"]).reshape(L * 512, 128),
        w_ba=np.ascontiguousarray(inp["w_br_a"]).reshape(L * D, D),
        w_bb=np.ascontiguousarray(inp["w_br_b"]).reshape(L * 512, D),
        w_out=np.ascontiguousarray(inp["w_out"]).reshape(L * D, D),
        w_up=np.ascontiguousarray(inp["w_up"]).reshape(L * D, 2 * DFF),
        w_dn=np.ascontiguousarray(inp["w_down"]).reshape(L * DFF, D),
    )
    maps = []
    for units, fl in zip(x_units, flags):
        xs = [u if u is not None else np.zeros((UNIT, D), np.float32) for u in units]
        xT = np.ascontiguousarray(np.concatenate(xs, axis=0).T)
        bands = np.stack([bF, bE if fl else bL, bE if fl else bF, bL, bE], axis=0)
        m = dict(shared)
        m["xT"] = xT
        m["bands"] = np.ascontiguousarray(bands.reshape(5 * 4 * 144, 128))
        m["flag"] = np.full((128, 1), 1.0 if fl else 0.0, np.float32)
        maps.append(m)
    return maps


_PROG_CACHE = {}


def kernel(**inputs):
    cfg = dict(CFG)
    inp = {k: np.asarray(v, dtype=np.float32) for k, v in inputs.items()}
    UNIT = cfg["UNIT"]
    xp = inp["x_prompt"][0]
    xsmp = inp["x_sample"]
    x_units = [[xp[0:UNIT], xp[UNIT:2 * UNIT]], [xsmp[0], xsmp[1]]] + [[xsmp[i], None] for i in range(2, 8)]
    flags = [True] + [False] * 7
    maps = host_inputs(inp, cfg, x_units, flags)
    key = (cfg["UNIT"], cfg["DEPTH"])
    if key not in _PROG_CACHE:
        _PROG_CACHE[key] = build_program(cfg)[0]
    nc = _PROG_CACHE[key]
    res = run_bass_kernel_spmd(nc, maps, core_ids=list(range(8)))
    ys = [np.asarray(r["Y"]) for r in res.results]
    y_prompt = np.ascontiguousarray(ys[0].T)[None]
    y_sample = np.empty_like(xsmp)
    y_sample[0] = ys[1][:, 0:UNIT].T
    y_sample[1] = ys[1][:, UNIT:2 * UNIT].T
    for i in range(2, 8):
        y_sample[i] = ys[i][:, 0:UNIT].T
    return (y_prompt.astype(np.float32), y_sample.astype(np.float32))
```

```python
import numpy as np
from contextlib import ExitStack
import concourse.bass as bass
import concourse.mybir as mybir
from concourse.bass_utils import run_bass_kernel_spmd
from concourse.ap import AP

F32 = mybir.dt.float32
BF16 = mybir.dt.bfloat16
ALU = mybir.AluOpType
AF = mybir.ActivationFunctionType

D = 1024
KC = 8
H = 4
DK = 128
DV = 256
INW = 5664
DFF = 2816
FC = 22
EPS = 1e-6
Q0, K0, V0, R0, LR0, U0, GA0, GB0 = 0, 512, 1024, 2048, 3072, 3104, 3616, 4640
POOL_WINDOWS = (2, 4, 8, 16)

CFG = dict(UNIT=8192, DEPTH=4, NCORES=8)


class Buf:
    __slots__ = ("name", "last_w", "readers")

    def __init__(self, name):
        self.name = name
        self.last_w = None
        self.readers = []


class Op:
    __slots__ = ("eng", "fn", "reads", "writes", "dma", "key", "kidx", "waits", "signal", "value", "barrier")

    def __init__(self, eng, fn, reads, writes, dma):
        self.eng = eng
        self.fn = fn
        self.reads = reads
        self.writes = writes
        self.dma = dma
        self.key = None
        self.kidx = 0
        self.waits = []
        self.signal = False
        self.value = 0
        self.barrier = False


ENGS = ("pe", "dve", "act", "pool", "sp")


class Prog:
    def __init__(self, nc):
        self.nc = nc
        self.ops = []
        self.kcount = {}

    def add(self, eng, fn, reads=(), writes=(), dma=None):
        op = Op(eng, fn, tuple(reads), tuple(writes), dma)
        op.key = ("dma:" + dma) if dma is not None else eng
        c = self.kcount.get(op.key, 0) + 1
        self.kcount[op.key] = c
        op.kidx = c
        self.ops.append(op)
        return op

    def pe(self, fn, reads=(), writes=()):
        return self.add("pe", fn, reads, writes)

    def dve(self, fn, reads=(), writes=()):
        return self.add("dve", fn, reads, writes)

    def act(self, fn, reads=(), writes=()):
        return self.add("act", fn, reads, writes)

    def pool(self, fn, reads=(), writes=()):
        return self.add("pool", fn, reads, writes)

    def dma(self, group, out, in_, reads=(), writes=(), slow=False):
        if slow:
            return self.add("sp", lambda e: e.dma_start(out=out, in_=in_, allow_slow_non_contiguous=True), reads, writes, dma=group)
        return self.add("sp", lambda e: e.dma_start(out=out, in_=in_), reads, writes, dma=group)

    def barrier(self):
        op = Op("sp", None, (), (), None)
        op.barrier = True
        self.ops.append(op)

    def analyze(self):
        known = {e: {} for e in ENGS}
        last_by_key = {}
        pending = {e: None for e in ENGS}
        for op in self.ops:
            if op.barrier:
                snap = dict(last_by_key)
                for e in ENGS:
                    pending[e] = snap
                continue
            deps = {}

            def need(p, kind):
                if p is None or p is op:
                    return
                if p.dma is None and op.dma is None and p.eng == op.eng:
                    if op.eng == "pe" or op.eng == "sp":
                        return
                    if kind != "raw":
                        return
                cur = deps.get(p.key)
                if cur is None or p.kidx > cur.kidx:
                    deps[p.key] = p

            for b in op.reads:
                need(b.last_w, "raw")
            for b in op.writes:
                need(b.last_w, "waw")
                for r in b.readers:
                    need(r, "war")
            if pending[op.eng] is not None:
                for k, p in pending[op.eng].items():
                    if p.dma is None and p.eng == op.eng and op.dma is None:
                        continue
                    cur = deps.get(k)
                    if cur is None or p.kidx > cur.kidx:
                        deps[k] = p
                pending[op.eng] = None
            kn = known[op.eng]
            for k, p in deps.items():
                if kn.get(k, 0) >= p.kidx:
                    continue
                kn[k] = p.kidx
                p.signal = True
                op.waits.append(p)
            for b in op.writes:
                b.last_w = op
                b.readers = []
            for b in op.reads:
                b.readers.append(op)
            if op.fn is not None:
                last_by_key[op.key] = op
        cnt = {}
        for op in self.ops:
            if op.signal:
                c = cnt.get(op.key, 0) + 1
                cnt[op.key] = c
                op.value = c * (16 if op.dma is not None else 1)
        self.sigcount = cnt

    def emit(self, stack):
        nc = self.nc
        self.analyze()
        sems = {}
        for k in self.sigcount:
            sems[k] = stack.enter_context(nc.semaphore("s_" + k.replace(":", "_")))
        per = {e: [] for e in ENGS}
        for op in self.ops:
            if not op.barrier:
                per[op.eng].append(op)
        block = stack.enter_context(nc.Block())

        def run(eng_name):
            def body(eng):
                for op in per[eng_name]:
                    for p in op.waits:
                        eng.wait_ge(sems[p.key], p.value)
                    if op.fn is not None:
                        ins = op.fn(eng)
                        if op.signal:
                            ins.then_inc(sems[op.key], 16 if op.dma is not None else 1)
            return body

        block.tensor(run("pe"))
        block.vector(run("dve"))
        block.scalar(run("act"))
        block.gpsimd(run("pool"))
        block.sync(run("sp"))
        return {k: len(v) for k, v in per.items()}, len(sems)


class Arena:
    def __init__(self, nc, stack, nbytes):
        self.t = stack.enter_context(nc.sbuf_tensor("arena", [128, nbytes // 2], BF16))
        self.cap = nbytes
        self.off = 0

    def alloc(self, shape, dt, parts=128, at=None):
        n = int(np.prod(shape))
        esz = 2 if dt == BF16 else 4
        nb = (n * esz + 63) // 64 * 64
        if at is None:
            o = self.off
            self.off += nb
            assert self.off <= self.cap, ("SBUF arena overflow", self.off, self.cap)
        else:
            o = at
        v = self.t[0:parts, o // 2: o // 2 + n * esz // 2]
        if dt == F32:
            v = v.bitcast(F32)
        if len(shape) == 2:
            v = v.rearrange("p (a b) -> p a b", a=shape[0])
        elif len(shape) == 3:
            v = v.rearrange("p (a b c) -> p a b c", a=shape[0], b=shape[1])
        return v


def rev2(v):
    (ps, pc), (es, en) = v.ap
    return AP(v.tensor, v.offset + (en - 1) * es, [[ps, pc], [-es, en]])


def small_layout(inp, L):
    def pp(a, nch):
        return np.ascontiguousarray(a.reshape(L, nch, 128).transpose(2, 0, 1).reshape(128, L * nch))
    parts = [pp(inp["norm1_g"], 8), pp(inp["norm2_g"], 8), pp(inp["onorm_g"], 8), pp(inp["pool_scale"], 4),
             pp(inp["b_lr_f"], 4), pp(inp["b_lr_b"], 4),
             np.ascontiguousarray(inp["conv_w"].reshape(L, 3, FC, 128).transpose(3, 0, 1, 2).reshape(128, L * 3 * FC)),
             pp(inp["conv_b"], FC),
             np.ascontiguousarray(inp["final_g"].reshape(8, 128).T)]
    return np.concatenate(parts, axis=1).astype(np.float32)


def sv_offsets(L):
    o = {}
    c = 0
    for name, n in [("g1", 8 * L), ("g2", 8 * L), ("og", 8 * L), ("ps", 4 * L), ("bf", 4 * L), ("bb", 4 * L),
                    ("cw", 3 * FC * L), ("cb", FC * L), ("fg", 8)]:
        o[name] = c
        c += n
    o["_n"] = c
    return o


def band_matrices(cls):
    out = np.zeros((4, 144, 128), np.float32)
    for gi, w in enumerate(POOL_WINDOWS):
        for t in range(128):
            lo = t - w // 2
            hi = t + (w - 1 - w // 2)
            if cls == "first":
                lo = max(lo, 0)
            if cls == "last":
                hi = min(hi, 127)
            cnt = hi - lo + 1
            for tp in range(lo, hi + 1):
                if tp < 0:
                    row = 128 + (tp + 8)
                elif tp > 127:
                    row = 136 + (tp - 128)
                else:
                    row = tp
                out[gi, row, t] += 1.0 / cnt
            out[gi, t, t] -= 1.0
    return out


def build_program(cfg):
    UNIT = cfg["UNIT"]
    L = cfg["DEPTH"]
    NT = 2 * UNIT
    NTF = NT // 512
    NCK = NT // 128
    TC = 256
    NTC = NT // TC
    SVO = sv_offsets(L)
    NS = SVO["_n"]

    nc = bass.Bass("TRN2", target_bir_lowering=False)

    def din(name, shape):
        return nc.dram_tensor(name, shape, F32, kind="ExternalInput").ap()

    xT = din("xT", [D, NT])
    svec = din("svec", [128, NS])
    cst = din("cst", [128, 4 * 128])
    bands = din("bands", [5 * 4 * 144, 128])
    flag_in = din("flag", [128, 1])
    w_in = din("w_in", [L * D, INW])
    wlr = din("wlr", [L * 32, 2 * 512])
    w_grp = din("w_grp", [L * 4 * 128, 128])
    w_ba = din("w_ba", [L * D, D])
    w_bb = din("w_bb", [L * 512, D])
    w_out = din("w_out", [L * D, D])
    w_up = din("w_up", [L * D, 2 * DFF])
    w_dn = din("w_dn", [L * DFF, D])
    Y = nc.dram_tensor("Y", [D, NT], F32, kind="ExternalOutput").ap()

    def dscr(name, shape, dt):
        return nc.dram_tensor(name, shape, dt).ap()

    Xs = dscr("Xs", [D, NT], F32)
    XM = dscr("XM", [D, NT + 2], F32)
    QB = dscr("QB", [512, NT], BF16)
    KB = dscr("KB", [NT, 512], BF16)
    Vs = dscr("Vs", [NT, 1024], BF16)
    Os = dscr("Os", [D, NT], BF16)
    Rs = dscr("Rs", [D, NT], BF16)
    GAs = dscr("GAs", [D, NT], BF16)
    GBs = dscr("GBs", [D, NT], BF16)
    Us = dscr("Us", [NT + 16, 512], BF16)

    P = Prog(nc)
    dbufs = {}

    def DB(name, t):
        k = (name, t)
        if k not in dbufs:
            dbufs[k] = Buf("%s%d" % (name, t))
        return dbufs[k]

    st = ExitStack()
    with st:
        ar = Arena(nc, st, 207 * 1024)
        pbank = [st.enter_context(nc.psum_tensor("pb%d" % i, [128, 512], F32)) for i in range(8)]
        pbuf = [Buf("pb%d" % i) for i in range(8)]

        cbf = ar.alloc([4, 128], BF16)
        ident = cbf[:, 0, :]
        ones = cbf[:, 1, :]
        mask_f = cbf[:, 2, :]
        mask_b = cbf[:, 3, :]
        zer = ar.alloc([128], F32)
        sv = ar.alloc([NS], F32)
        negb = ar.alloc([8 * L], F32)
        flag = ar.alloc([1], F32)
        c_start = ar.off
        S32 = ar.alloc([H, DV], F32)
        Sbf = ar.alloc([H, DV], BF16)
        abfirst = ar.alloc([H, NCK], F32)
        akeep = ar.alloc([H], F32)
        B_akeep = Buf('akeep')
        B_const = Buf("const")
        B_S32 = [Buf("S32_%d" % h) for h in range(H)]
        B_Sbf = [Buf("Sbf_%d" % h) for h in range(H)]
        B_abf = Buf("abfirst")
        persist_end = ar.off

        stg0 = ar.alloc([4 * 128], F32)
        P.dma("c0", stg0, cst, writes=[B_const])
        P.dve(lambda e: e.tensor_copy(cbf.rearrange("p a b -> p (a b)"), stg0), [B_const], [B_const])
        P.dma("c1", sv, svec, writes=[B_const])
        P.dma("c2", flag, flag_in, writes=[B_const])
        P.pool(lambda e: e.memset(zer, 0.0), [], [B_const])
        P.dve(lambda e: e.tensor_scalar(negb, sv[:, SVO["bf"]:SVO["bf"] + 8 * L], -1.0, None, ALU.mult), [B_const], [B_const])
        zb = ar.alloc([8, 512], BF16)
        P.pool(lambda e: e.memset(zb, 0.0), [], [B_const])
        zb3 = zb.bitcast(F32) if False else None
        zf = ar.alloc([8, 1], F32)
        P.pool(lambda e: e.memset(zf, 0.0), [], [B_const])
        P.dma("g0", Us[0:8, :], zb[0:8, 0, :], reads=[B_const], writes=[DB("U", -1)])
        P.dma("g1", Us[NT + 8:NT + 16, :], zb[0:8, 0, :], reads=[B_const], writes=[DB("U", NTF)])
        P.dma("g2", XM[:, 0:1].rearrange("(k p) n -> p k n", p=128), zf, reads=[B_const], writes=[DB("XM", -1)], slow=True)
        P.dma("g3", XM[:, NT + 1:NT + 2].rearrange("(k p) n -> p k n", p=128), zf, reads=[B_const], writes=[DB("XM", NTF)], slow=True)
        P.barrier()
        ar.off = persist_end

        rr = [0]

        def load_w(dst, src, nk, ncols, scale_col=None, cuts=(), muls=None, stg=None, stgB=None):
            edges = sorted(set([0, ncols] + [c for c in cuts if 0 < c < ncols]))
            blocks = []
            for a, b in zip(edges[:-1], edges[1:]):
                c = a
                while c < b:
                    blocks.append((c, min(b, c + 2048)))
                    c += 2048
            for k in range(nk):
                for (c0, c1) in blocks:
                    i = rr[0] % 2
                    rr[0] += 1
                    s = stg[i][:, 0:c1 - c0]
                    P.dma("stg%d" % i, s, src[k * 128:(k + 1) * 128, c0:c1], writes=[stgB[i]])
                    o = dst[:, k, c0:c1]
                    mul = None
                    if muls:
                        for (m0, m1, mv) in muls:
                            if c0 >= m0 and c1 <= m1:
                                mul = mv
                    sc = sv[:, scale_col + k:scale_col + k + 1] if scale_col is not None else None
                    eng = ("act", "dve")[rr[0] % 2]
                    if mul is not None and eng == "act":
                        eng = "dve"
                    if eng == "act":
                        if sc is None:
                            P.act(lambda e, o=o, s=s: e.copy(o, s), [stgB[i]], [])
                        else:
                            P.act(lambda e, o=o, s=s, sc=sc: e.activation(o, s, AF.Identity, scale=sc), [stgB[i], B_const], [])
                    else:
                        if sc is None and mul is None:
                            f = lambda e, o=o, s=s: e.tensor_copy(o, s)
                        elif mul is None:
                            f = lambda e, o=o, s=s, sc=sc: e.tensor_scalar(o, s, sc, None, ALU.mult)
                        elif sc is None:
                            f = lambda e, o=o, s=s, mul=mul: e.tensor_scalar(o, s, float(mul), None, ALU.mult)
                        else:
                            f = lambda e, o=o, s=s, sc=sc, mul=mul: e.tensor_scalar(o, s, sc, float(mul), ALU.mult, ALU.mult)
                        P.add(eng, f, [stgB[i], B_const], [])

        def mm(out, lhsT, rhs, start, stop, reads, writes):
            P.pe(lambda e: e.matmul(out, lhsT, rhs, start=start, stop=stop), reads, writes)

        for l in range(L):
            last_layer = (l == L - 1)
            Xsrc = xT if l == 0 else Xs
            ar.off = persist_end
            WIN = ar.alloc([KC, INW], BF16)
            WLR = ar.alloc([2, 512], BF16, parts=32)
            xt = ar.alloc([KC, 512], F32)
            xn = ar.alloc([KC, 512], BF16)
            lnt = ar.alloc([512], F32)
            rstd = ar.alloc([512], F32)
            ring_off = ar.off
            ring = [ar.alloc([4, 512], BF16) for _ in range(4)]
            stg = [ar.alloc([2048], F32, at=ring_off), ar.alloc([2048], F32, at=ring_off + 8192)]
            sq = ar.alloc([KC, 512], BF16)
            lrT = ar.alloc([512], BF16, parts=32)
            vtm = ar.alloc([4, 1024], BF16)
            utm = ar.alloc([4, 512], BF16)
            kbstg = ar.alloc([4, 512], BF16)
            ost = [ar.alloc([2, 512], BF16) for _ in range(2)]
            qbs = [ar.alloc([512], BF16) for _ in range(2)]
            el = ar.alloc([512], F32)
            bc = ar.alloc([512], F32)
            Asets = [[ar.alloc([512], F32) for _ in range(4)] for _ in range(2)]
            Anames = [["Af%d" % i_, "Aif%d" % i_, "Ab%d" % i_, "Aib%d" % i_] for i_ in range(2)]
            qf = ar.alloc([512], BF16)
            kf = ar.alloc([512], BF16)
            kbT = ar.alloc([512], BF16)
            ktm4 = ar.alloc([4, 128], BF16)
            B_P4 = Buf('P4')
            sTf = [ar.alloc([128], BF16) for _ in range(2)]
            sTb = [ar.alloc([128], BF16) for _ in range(2)]
            B = {n: Buf("F_" + n) for n in ("xt xn lnt rstd sq lrT vtm utm kbstg el bc Af0 Aif0 Ab0 Aib0 Af1 Aif1 Ab1 Aib1 qf kf kbT "
                                          "ring0 ring1 ring2 ring3 ost0 ost1 qbs0 qbs1 ktm4 sTf0 sTf1 sTb0 sTb1 stg0 stg1").split()}
            stgB = [B["stg0"], B["stg1"]]
            load_w(WIN, w_in[l * D:(l + 1) * D, :], KC, INW, scale_col=SVO["g1"] + 8 * l, cuts=(512,),
                   muls=[(0, 512, DK ** -0.5)], stg=stg, stgB=stgB)
            P.dma("stg0", stg[0][0:32, 0:1024], wlr[l * 32:(l + 1) * 32, :], writes=[stgB[0]])
            P.dve(lambda e: e.tensor_copy(WLR.rearrange("p a b -> p (a b)"), stg[0][0:32, 0:1024]), [stgB[0]], [])
            P.barrier()
            pjr = [0]
            ringi = [0]

            def next_pj():
                i = pjr[0] % 3
                pjr[0] += 1
                return i

            for t in range(NTF):
                t0 = t * 512
                unit_start = (t0 % UNIT == 0)
                if t0 == 0:
                    P.dve(lambda e: e.memset(akeep, 0.0), [], [B_akeep])
                    for h in range(H):
                        P.dve(lambda e, h=h: e.memset(S32[:, h, :], 0.0), [], [B_S32[h]])
                        P.dve(lambda e, h=h: e.memset(Sbf[:, h, :], 0.0), [], [B_Sbf[h]])
                elif unit_start:
                    for h in range(H):
                        P.dve(lambda e, h=h: e.tensor_scalar(S32[:, h, :], S32[:, h, :], flag[:, 0:1], None, ALU.mult),
                              [B_S32[h], B_const], [B_S32[h]])
                        P.dve(lambda e, h=h: e.tensor_scalar(Sbf[:, h, :], Sbf[:, h, :], flag[:, 0:1], None, ALU.mult),
                              [B_Sbf[h], B_const], [B_Sbf[h]])
                if t == 0:
                    P.dma("F_xt", xt, Xsrc[:, t0:t0 + 512].rearrange("(k p) n -> p k n", p=128),
                          reads=[DB("X", t)], writes=[B["xt"]])
                P.act(lambda e: e.activation(sq, xt, AF.Square), [B["xt"]], [B["sq"]])
                for k in range(KC):
                    mm(pbank[3][:, :], ones, sq[:, k, :], k == 0, k == KC - 1, [B["sq"], B_const], [pbuf[3]])
                P.act(lambda e: e.activation(lnt, pbank[3][:, :], AF.Ln, scale=1.0 / D, bias=EPS), [pbuf[3]], [B["lnt"]])
                P.act(lambda e: e.activation(rstd, lnt, AF.Exp, scale=-0.5), [B["lnt"]], [B["rstd"]])
                P.dve(lambda e: e.tensor_tensor(xn, xt, rstd.unsqueeze(1).broadcast_to([128, KC, 512]), ALU.mult),
                      [B["xt"], B["rstd"]], [B["xn"]])
                if t + 1 < NTF:
                    P.dma("F_xt", xt, Xsrc[:, t0 + 512:t0 + 1024].rearrange("(k p) n -> p k n", p=128),
                          reads=[DB("X", t + 1)], writes=[B["xt"]])
                pj = next_pj()
                for k in range(KC):
                    mm(pbank[pj][0:32, :], WIN[:, k, LR0:LR0 + 32], xn[:, k, :], k == 0, k == KC - 1, [B["xn"]], [pbuf[pj]])
                P.act(lambda e, pj=pj: e.copy(lrT, pbank[pj][0:32, :]), [pbuf[pj]], [B["lrT"]])
                def rgg_groups(t=t, t0=t0):
                    for (col0, dst, dname) in ((R0, Rs, "R"), (GA0, GAs, "GA"), (GB0, GBs, "GB")):
                        for half in range(2):
                            ri = ringi[0] % 4
                            ringi[0] += 1
                            rb = ring[ri]
                            for j in range(4):
                                c = half * 4 + j
                                pj = next_pj()
                                for k in range(KC):
                                    mm(pbank[pj][:, :], WIN[:, k, col0 + c * 128:col0 + (c + 1) * 128], xn[:, k, :],
                                       k == 0, k == KC - 1, [B["xn"]], [pbuf[pj]])
                                P.act(lambda e, pj=pj, rb=rb, j=j: e.copy(rb[:, j, :], pbank[pj][:, :]),
                                      [pbuf[pj]], [B["ring%d" % ri]])
                                if j == 3:
                                    P.dma("F_ring%d" % ri,
                                          dst[half * 512:(half + 1) * 512, t0:t0 + 512].rearrange("(k p) n -> p k n", p=128), rb,
                                          reads=[B["ring%d" % ri]], writes=[DB(dname, t)])
                                yield
                rgg = rgg_groups()

                def rgg_step(n=1):
                    for _ in range(n):
                        next(rgg, None)
                for tc in range(4):
                    for cg in range(2):
                        pj = next_pj()
                        for k in range(KC):
                            mm(pbank[pj][:, :], xn[:, k, tc * 128:(tc + 1) * 128], WIN[:, k, V0 + cg * 512:V0 + (cg + 1) * 512],
                               k == 0, k == KC - 1, [B["xn"]], [pbuf[pj]])
                        P.dve(lambda e, pj=pj, tc=tc, cg=cg: e.tensor_copy(vtm[:, tc, cg * 512:(cg + 1) * 512], pbank[pj][:, :]),
                              [pbuf[pj]], [B["vtm"]])
                P.dma("F_vtm", Vs[t0:t0 + 512, :].rearrange("(c p) n -> p c n", p=128), vtm,
                      reads=[B["vtm"]], writes=[DB("V", t)])
                for tc in range(4):
                    pj = next_pj()
                    for k in range(KC):
                        mm(pbank[pj][:, :], xn[:, k, tc * 128:(tc + 1) * 128], WIN[:, k, U0:U0 + 512],
                           k == 0, k == KC - 1, [B["xn"]], [pbuf[pj]])
                    P.dve(lambda e, pj=pj, tc=tc: e.tensor_copy(utm[:, tc, :], pbank[pj][:, :]), [pbuf[pj]], [B["utm"]])
                P.dma("F_utm", Us[8 + t0:8 + t0 + 512, :].rearrange("(c p) n -> p c n", p=128), utm,
                      reads=[B["utm"]], writes=[DB("U", t)])
                def decays(h, A4, A4n):
                    for dr in range(2):
                        A_, Ai_ = A4[2 * dr], A4[2 * dr + 1]
                        An, Ain = A4n[2 * dr], A4n[2 * dr + 1]
                        mm(pbank[3][:, :], WLR[0:32, dr, h * 128:(h + 1) * 128], lrT[0:32, :], True, True, [B["lrT"]], [pbuf[3]])
                        nb_ = negb[:, (dr * L + l) * 4 + h:(dr * L + l) * 4 + h + 1]
                        P.act(lambda e, nb_=nb_: e.activation(el, pbank[3][:, :], AF.Exp, scale=-1.0, bias=nb_),
                              [pbuf[3], B_const], [B["el"]])
                        P.act(lambda e: e.activation(el, el, AF.Ln, bias=1.0), [B["el"]], [B["el"]])
                        for c in range(4):
                            if dr == 0:
                                P.dve(lambda e, c=c: e.tensor_tensor_scan(bc[:, c * 128:(c + 1) * 128], el[:, c * 128:(c + 1) * 128],
                                                                          zer, 0.0, ALU.add, ALU.add),
                                      [B["el"], B_const], [B["bc"]])
                            else:
                                P.dve(lambda e, c=c: e.tensor_tensor_scan(rev2(bc[:, c * 128:(c + 1) * 128]),
                                                                          rev2(el[:, c * 128:(c + 1) * 128]),
                                                                          zer, 0.0, ALU.add, ALU.add),
                                      [B["el"], B_const], [B["bc"]])
                        P.act(lambda e, A_=A_: e.activation(A_, bc, AF.Exp, scale=-1.0 / 16), [B["bc"]], [B[An]])
                        P.act(lambda e, Ai_=Ai_: e.activation(Ai_, bc, AF.Exp, scale=1.0 / 16), [B["bc"]], [B[Ain]])

                decays(0, Asets[0], Anames[0])
                for h in range(H):
                    Af, Aif, Ab, Aib = Asets[h % 2]
                    nAf, nAif, nAb, nAib = Anames[h % 2]
                    pq = next_pj()
                    for k in range(KC):
                        mm(pbank[pq][:, :], WIN[:, k, Q0 + h * 128:Q0 + (h + 1) * 128], xn[:, k, :], k == 0, k == KC - 1,
                           [B["xn"]], [pbuf[pq]])
                    pk = next_pj()
                    for k in range(KC):
                        mm(pbank[pk][:, :], WIN[:, k, K0 + h * 128:K0 + (h + 1) * 128], xn[:, k, :], k == 0, k == KC - 1,
                           [B["xn"]], [pbuf[pk]])
                    qi = h % 2
                    qb_ = qbs[qi]
                    P.dve(lambda e, pq=pq, Af=Af: e.tensor_tensor(qf, pbank[pq][:, :], Af, ALU.mult), [pbuf[pq], B[nAf]], [B["qf"]])
                    P.dve(lambda e, pq=pq, qb_=qb_, Ab=Ab: e.tensor_tensor(qb_, pbank[pq][:, :], Ab, ALU.mult),
                          [pbuf[pq], B[nAb]], [B["qbs%d" % qi]])
                    P.dve(lambda e, pk=pk, Aif=Aif: e.tensor_tensor(kf, pbank[pk][:, :], Aif, ALU.mult), [pbuf[pk], B[nAif]], [B["kf"]])
                    P.dve(lambda e, pk=pk, Aib=Aib: e.tensor_tensor(kbT, pbank[pk][:, :], Aib, ALU.mult), [pbuf[pk], B[nAib]], [B["kbT"]])
                    P.dma("F_qbs%d" % qi, QB[h * 128:(h + 1) * 128, t0:t0 + 512], qb_, reads=[B["qbs%d" % qi]], writes=[DB("QB", t)])
                    P.pool(lambda e, h=h, t=t, Ab=Ab: e.tensor_copy(abfirst[:, h, t * 4:(t + 1) * 4],
                                                              Ab.rearrange("p (c j) -> p c j", j=128)[:, :, 0]),
                           [B[nAb]], [B_abf])
                    oi = h % 2
                    trv = pbank[7].bitcast(BF16)
                    for c in range(4):
                        cs = slice(c * 128, (c + 1) * 128)
                        P.pe(lambda e, c=c, cs=cs: e.transpose(trv[:, c * 128:(c + 1) * 128], kf[:, cs], ident), [B["kf"], B_const], [pbuf[7]])
                        P.pe(lambda e, c=c, cs=cs: e.transpose(trv[:, 512 + c * 128:512 + (c + 1) * 128], kbT[:, cs], ident), [B["kbT"], B_const], [pbuf[7]])
                    P.act(lambda e: e.copy(ktm4, trv[:, 0:512].rearrange("p (c j) -> p c j", j=128)), [pbuf[7]], [B["ktm4"]])
                    P.act(lambda e, h=h: e.copy(kbstg[:, :, h * 128:(h + 1) * 128], trv[:, 512:1024].rearrange("p (c j) -> p c j", j=128)),
                          [pbuf[7]], [B["kbstg"]])
                    if h + 1 < H:
                        decays(h + 1, Asets[(h + 1) % 2], Anames[(h + 1) % 2])
                    for c in range(4):
                        cs = slice(c * 128, (c + 1) * 128)
                        si = c % 2
                        scb_ = pbuf[4]
                        mm(pbank[4][:, 0:128], kf[:, cs], qf[:, cs], True, True, [B["kf"], B["qf"]], [scb_])
                        mm(pbank[4][:, 128:256], kbT[:, cs], qb_[:, cs], True, True,
                           [B["kbT"], B["qbs%d" % qi]], [scb_])
                        P.dve(lambda e, si=si: e.tensor_tensor(sTf[si], pbank[4][:, 0:128], mask_f, ALU.mult),
                              [scb_, B_const], [B["sTf%d" % si]])
                        P.dve(lambda e, si=si: e.tensor_tensor(sTb[si], pbank[4][:, 128:256], mask_b, ALU.mult),
                              [scb_, B_const], [B["sTb%d" % si]])
                        rgg_step(1)
                        for vc in range(2):
                            vcol = h * 256 + vc * 128
                            ob = 5 + vc
                            mm(pbank[ob][:, cs], vtm[:, c, vcol:vcol + 128], sTf[si], True, False,
                               [B["vtm"], B["sTf%d" % si]], [pbuf[ob]])
                            mm(pbank[ob][:, cs], vtm[:, c, vcol:vcol + 128], sTb[si], False, False,
                               [B["vtm"], B["sTb%d" % si]], [pbuf[ob]])
                            mm(pbank[ob][:, cs], Sbf[:, h, vc * 128:(vc + 1) * 128], qf[:, cs], False, True,
                               [B_Sbf[h], B["qf"]], [pbuf[ob]])
                        mm(pbank[4][:, 256:512], ktm4[:, c, :], vtm[:, c, h * 256:(h + 1) * 256], True, True,
                           [B["ktm4"], B["vtm"]], [B_P4])
                        aprev = akeep[:, h:h + 1] if c == 0 else Af[:, (c - 1) * 128 + 127:(c - 1) * 128 + 128]
                        P.dve(lambda e, h=h, aprev=aprev: e.scalar_tensor_tensor(S32[:, h, :], S32[:, h, :], aprev, pbank[4][:, 256:512],
                                                                                 ALU.mult, ALU.add),
                              [B_P4, B_S32[h], B[nAf], B_akeep], [B_S32[h]])
                        alast = Af[:, c * 128 + 127:c * 128 + 128]
                        P.dve(lambda e, h=h, alast=alast: e.tensor_scalar(Sbf[:, h, :], S32[:, h, :], alast, None, ALU.mult),
                              [B_S32[h], B[nAf]], [B_Sbf[h]])
                        if c == 3:
                            P.dve(lambda e, h=h, alast=alast: e.tensor_copy(akeep[:, h:h + 1], alast), [B[nAf]], [B_akeep])
                        if c % 2 == 1:
                            rgg_step(1)
                    os_ = ost[oi]
                    P.act(lambda e, os_=os_: e.copy(os_[:, 0, :], pbank[5][:, :]), [pbuf[5]], [B["ost%d" % oi]])
                    P.dve(lambda e, os_=os_: e.tensor_copy(os_[:, 1, :], pbank[6][:, :]), [pbuf[6]], [B["ost%d" % oi]])
                    P.dma("F_ost%d" % oi, Os[h * 256:(h + 1) * 256, t0:t0 + 512].rearrange("(k p) n -> p k n", p=128), os_,
                          reads=[B["ost%d" % oi]], writes=[DB("O", t)])
                rgg_step(24)
                P.dma("F_kbstg", KB[t0:t0 + 512, :].rearrange("(c p) n -> p c n", p=128), kbstg,
                      reads=[B["kbstg"]], writes=[DB("KB", t)])
            P.barrier()

            ar.off = persist_end
            WBA = ar.alloc([KC, D], BF16)
            WBB = ar.alloc([4, D], BF16)
            WOUT = ar.alloc([KC, D], BF16)
            WGRP = ar.alloc([4, 128], BF16)
            bandc = ar.alloc([20, 128], BF16)
            bandh = ar.alloc([20, 128], BF16, parts=16)
            Sb32 = ar.alloc([H, DV], F32)
            Sbb = ar.alloc([H, DV], BF16)
            rt = ar.alloc([KC, 512], BF16)
            gat = ar.alloc([KC, 512], BF16)
            gbt = ar.alloc([KC, 512], BF16)
            sets = []
            for si_ in range(2):
                o_ = ar.off
                ot_ = ar.alloc([KC, 512], BF16)
                m2_ = ar.alloc([KC, 512], BF16, at=o_)
                o_ = ar.off
                qbt_ = ar.alloc([H, 512], BF16)
                pbf_ = ar.alloc([4, 512], BF16, at=o_)
                o_ = ar.off
                kbt_ = ar.alloc([4, 512], BF16)
                p2bf_ = ar.alloc([4, 512], BF16, at=o_)
                o_ = ar.off
                vt_ = ar.alloc([4, 1024], BF16)
                m1_ = ar.alloc([KC, 512], BF16, at=o_)
                sets.append(dict(ot=ot_, m2=m2_, qbt=qbt_, pbf=pbf_, kbt=kbt_, p2bf=p2bf_, vt=vt_, m1=m1_,
                                 Bot=Buf("B_ot%d" % si_), Bqbt=Buf("B_qbt%d" % si_), Bkbt=Buf("B_kbt%d" % si_), Bvt=Buf("B_vt%d" % si_)))
            ut = ar.alloc([4, 512], BF16)
            uh = ar.alloc([4, 512], BF16, parts=16)
            xtb = ar.alloc([KC, 512], F32)
            stg_off = ar.off
            osum = ar.alloc([KC, 512], F32)
            stgb = [ar.alloc([2048], F32, at=stg_off), ar.alloc([2048], F32, at=stg_off + 8192)]
            sqy = ar.alloc([KC, 512], BF16)
            rstd4 = ar.alloc([H, 512], F32)
            lnb = ar.alloc([512], F32)
            B = {n: Buf("B_" + n) for n in ("rt gat gbt ut uh xtb osum sqy rstd4 lnb stg0 stg1").split()}
            B_Sb32 = [Buf("Sb32_%d" % h) for h in range(H)]
            B_Sbb = [Buf("Sbb_%d" % h) for h in range(H)]
            stgB = [B["stg0"], B["stg1"]]
            load_w(WBA, w_ba[l * D:(l + 1) * D, :], KC, D, scale_col=SVO["og"] + 8 * l, stg=stgb, stgB=stgB)
            load_w(WBB, w_bb[l * 512:(l + 1) * 512, :], 4, D, scale_col=SVO["ps"] + 4 * l, stg=stgb, stgB=stgB)
            load_w(WOUT, w_out[l * D:(l + 1) * D, :], KC, D, stg=stgb, stgB=stgB)
            load_w(WGRP, w_grp[l * 512:(l + 1) * 512, :], 4, 128, stg=stgb, stgB=stgB)
            bsrc = bands.rearrange("(m r) n -> r m n", r=144)
            for half in range(2):
                i = rr[0] % 2
                rr[0] += 1
                s3 = stgb[i][:, 0:1280].rearrange("p (m n) -> p m n", n=128)
                P.dma("stg%d" % i, s3, bsrc[0:128, half * 10:(half + 1) * 10, :], writes=[stgB[i]])
                P.dve(lambda e, s3=s3, half=half: e.tensor_copy(bandc[:, half * 10:(half + 1) * 10, :], s3), [stgB[i]], [])
                i = rr[0] % 2
                rr[0] += 1
                s3h = stgb[i][0:16, 0:1280].rearrange("p (m n) -> p m n", n=128)
                P.dma("stg%d" % i, s3h, bsrc[128:144, half * 10:(half + 1) * 10, :], writes=[stgB[i]])
                P.dve(lambda e, s3h=s3h, half=half: e.tensor_copy(bandh[:, half * 10:(half + 1) * 10, :], s3h), [stgB[i]], [])
            P.barrier()
            pjr = [0]

            def next_pj():
                i = pjr[0] % 3
                pjr[0] += 1
                return i

            def load_set(tt, S_):
                tt0 = tt * 512
                P.dma("B_qbt%d" % S_["i"], S_["qbt"], QB[:, tt0:tt0 + 512].rearrange("(k p) n -> p k n", p=128), reads=[DB("QB", tt)], writes=[S_["Bqbt"]])
                P.dma("B_kbt%d" % S_["i"], S_["kbt"], KB[tt0:tt0 + 512, :].rearrange("(c p) n -> p c n", p=128), reads=[DB("KB", tt)], writes=[S_["Bkbt"]])
                P.dma("B_vt%d" % S_["i"], S_["vt"], Vs[tt0:tt0 + 512, :].rearrange("(c p) n -> p c n", p=128), reads=[DB("V", tt)], writes=[S_["Bvt"]])
                P.dma("B_ot%d" % S_["i"], S_["ot"], Os[:, tt0:tt0 + 512].rearrange("(k p) n -> p k n", p=128), reads=[DB("O", tt)], writes=[S_["Bot"]])

            sets[0]["i"] = 0
            sets[1]["i"] = 1
            load_set(NTF - 1, sets[0])
            for t in range(NTF - 1, -1, -1):
                t0 = t * 512
                S_ = sets[(NTF - 1 - t) % 2]
                ot, m2, qbt, pbf, kbt, p2bf, vt, m1 = (S_[k_] for k_ in ("ot", "m2", "qbt", "pbf", "kbt", "p2bf", "vt", "m1"))
                B["ot"] = B["m2"] = S_["Bot"]
                B["qbt"] = B["pbf"] = S_["Bqbt"]
                B["kbt"] = B["p2bf"] = S_["Bkbt"]
                B["vt"] = B["m1"] = S_["Bvt"]
                if t > 0:
                    load_set(t - 1, sets[(NTF - t) % 2])
                if t == NTF - 1:
                    for h in range(H):
                        P.dve(lambda e, h=h: e.memset(Sb32[:, h, :], 0.0), [], [B_Sb32[h]])
                        P.dve(lambda e, h=h: e.memset(Sbb[:, h, :], 0.0), [], [B_Sbb[h]])
                elif (t0 + 512) % UNIT == 0:
                    for h in range(H):
                        P.dve(lambda e, h=h: e.tensor_scalar(Sb32[:, h, :], Sb32[:, h, :], flag[:, 0:1], None, ALU.mult),
                              [B_Sb32[h], B_const], [B_Sb32[h]])
                        P.dve(lambda e, h=h: e.tensor_scalar(Sbb[:, h, :], Sbb[:, h, :], flag[:, 0:1], None, ALU.mult),
                              [B_Sbb[h], B_const], [B_Sbb[h]])
                fm = lambda T_: T_[:, t0:t0 + 512].rearrange("(k p) n -> p k n", p=128)
                P.dma("B_rt", rt, fm(Rs), reads=[DB("R", t)], writes=[B["rt"]])
                P.dma("B_gat", gat, fm(GAs), reads=[DB("GA", t)], writes=[B["gat"]])
                P.dma("B_gbt", gbt, fm(GBs), reads=[DB("GB", t)], writes=[B["gbt"]])
                P.dma("B_ut", ut, Us[8 + t0:8 + t0 + 512, :].rearrange("(c p) n -> p c n", p=128), reads=[DB("U", t)], writes=[B["ut"]])
                P.dma("B_uh", uh[0:8], Us[t0:t0 + 512, :].rearrange("(c p) n -> p c n", p=128)[0:8],
                      reads=[DB("U", t - 1), DB("U", t)], writes=[B["uh"]])
                uh2src = AP(Us.tensor, Us.offset + (8 + t0 + 128) * 512, [[512, 8], [128 * 512, 4], [1, 512]])
                P.dma("B_uh2", uh[8:16], uh2src,
                      reads=[DB("U", t + 1), DB("U", t)], writes=[B["uh"]])
                P.dma("B_xtb", xtb, fm(Xsrc), reads=[DB("X", t)], writes=[B["xtb"]])
                P.act(lambda e: e.activation(sqy, rt, AF.Sigmoid), [B["rt"]], [B["sqy"]])
                P.pool(lambda e: e.tensor_tensor(rt, rt, sqy, ALU.mult), [B["rt"], B["sqy"]], [B["rt"]])
                P.act(lambda e: e.activation(gat, gat, AF.Sigmoid), [B["gat"]], [B["gat"]])
                P.act(lambda e: e.activation(gbt, gbt, AF.Sigmoid), [B["gbt"]], [B["gbt"]])
                osum4 = osum.rearrange("p (h v) n -> p h v n", v=2)
                ot4 = ot.rearrange("p (h v) n -> p h v n", v=2)
                for c in range(3, -1, -1):
                    cs = slice(c * 128, (c + 1) * 128)
                    obk = (5, 6) if c % 2 == 1 else (0, 1)
                    gck = t * 4 + c
                    gprev = min(gck + 1, NCK - 1)
                    for h in range(H):
                        for vc in range(2):
                            mm(pbank[obk[vc]][:, h * 128:(h + 1) * 128], Sbb[:, h, vc * 128:(vc + 1) * 128], qbt[:, h, cs], True, True,
                               [B_Sbb[h], B["qbt"]], [pbuf[obk[vc]]])
                        pb_, pc_ = (7, (h % 2) * 256) if h < 2 else (4, (h % 2) * 256)
                        mm(pbank[pb_][:, pc_:pc_ + 256], kbt[:, c, h * 128:(h + 1) * 128], vt[:, c, h * 256:(h + 1) * 256], True, True,
                           [B["kbt"], B["vt"]], [pbuf[pb_]])
                        aprev = abfirst[:, h, gprev:gprev + 1]
                        P.dve(lambda e, h=h, aprev=aprev, pb_=pb_, pc_=pc_: e.scalar_tensor_tensor(
                            Sb32[:, h, :], Sb32[:, h, :], aprev, pbank[pb_][:, pc_:pc_ + 256], ALU.mult, ALU.add),
                              [pbuf[pb_], B_Sb32[h], B_abf], [B_Sb32[h]])
                        afirst = abfirst[:, h, gck:gck + 1]
                        P.dve(lambda e, h=h, afirst=afirst: e.tensor_scalar(Sbb[:, h, :], Sb32[:, h, :], afirst, None, ALU.mult),
                              [B_Sb32[h], B_abf], [B_Sbb[h]])
                    for vc in range(2):
                        P.dve(lambda e, vc=vc, cs=cs, obk=obk, ot4=ot4: e.tensor_tensor(
                            osum4[:, :, vc, cs], pbank[obk[vc]][:, :].rearrange("p (h j) -> p h j", j=128), ot4[:, :, vc, cs], ALU.add),
                              [pbuf[obk[vc]], B["ot"]], [B["osum"]])
                def pool_branch():
                    for g in range(4):
                        for c in range(4):
                            gck = t * 4 + c
                            pos = gck % (UNIT // 128)
                            if pos == 0:
                                cls = 0 if gck == 0 else 2
                            elif pos == UNIT // 128 - 1:
                                cls = 1 if gck < UNIT // 128 else 3
                            else:
                                cls = 4
                            cs = slice(c * 128, (c + 1) * 128)
                            mm(pbank[4][:, cs], ut[:, c, g * 128:(g + 1) * 128], bandc[:, cls * 4 + g, :], True, False, [B["ut"]], [pbuf[4]])
                            mm(pbank[4][:, cs], uh[0:16, c, g * 128:(g + 1) * 128], bandh[0:16, cls * 4 + g, :], False, True, [B["uh"]], [pbuf[4]])
                        P.act(lambda e, g=g, pbf=pbf: e.copy(pbf[:, g, :], pbank[4][:, :]), [pbuf[4]], [B["pbf"]])
                        pj = next_pj()
                        mm(pbank[pj][:, :], WGRP[:, g, :], pbf[:, g, :], True, True, [B["pbf"]], [pbuf[pj]])
                        P.act(lambda e, g=g, pj=pj, p2bf=p2bf: e.copy(p2bf[:, g, :], pbank[pj][:, :]), [pbuf[pj]], [B["p2bf"]])
                    for oc in range(KC):
                        pj = next_pj()
                        for g in range(4):
                            mm(pbank[pj][:, :], WBB[:, g, oc * 128:(oc + 1) * 128], p2bf[:, g, :], g == 0, g == 3, [B["p2bf"]], [pbuf[pj]])
                        P.dve(lambda e, pj=pj, oc=oc, m2=m2: e.tensor_tensor(m2[:, oc, :], pbank[pj][:, :], gbt[:, oc, :], ALU.mult),
                              [pbuf[pj], B["gbt"]], [B["m2"]])
                P.act(lambda e: e.activation(sqy, osum, AF.Square), [B["osum"]], [B["sqy"]])
                pool_branch()
                for h in range(H):
                    for vc in range(2):
                        mm(pbank[3][:, :], ones, sqy[:, h * 2 + vc, :], vc == 0, vc == 1, [B["sqy"], B_const], [pbuf[3]])
                    P.act(lambda e: e.activation(lnb, pbank[3][:, :], AF.Ln, scale=1.0 / DV, bias=EPS), [pbuf[3]], [B["lnb"]])
                    P.act(lambda e, h=h: e.activation(rstd4[:, h, :], lnb, AF.Exp, scale=-0.5), [B["lnb"]], [B["rstd4"]])
                P.dve(lambda e: e.tensor_tensor(osum4, osum4, rstd4.unsqueeze(2).broadcast_to([128, H, 2, 512]), ALU.mult),
                      [B["osum"], B["rstd4"]], [B["osum"]])
                P.dve(lambda e: e.tensor_tensor(sqy, osum, rt, ALU.mult), [B["osum"], B["rt"], B["sqy"]], [B["sqy"]])
                for oc in range(KC):
                    pj = next_pj()
                    for k in range(KC):
                        mm(pbank[pj][:, :], WBA[:, k, oc * 128:(oc + 1) * 128], sqy[:, k, :], k == 0, k == KC - 1, [B["sqy"]], [pbuf[pj]])
                    P.dve(lambda e, pj=pj, oc=oc, m1=m1: e.tensor_tensor(m1[:, oc, :], pbank[pj][:, :], gat[:, oc, :], ALU.mult),
                          [pbuf[pj], B["gat"]], [B["m1"]])
                P.dve(lambda e, m1=m1, m2=m2: e.tensor_tensor(m1, m1, m2, ALU.add), [B["m1"], B["m2"]], [B["m1"]])
                for oc in range(KC):
                    pj = next_pj()
                    for k in range(KC):
                        mm(pbank[pj][:, :], WOUT[:, k, oc * 128:(oc + 1) * 128], m1[:, k, :], k == 0, k == KC - 1, [B["m1"]], [pbuf[pj]])
                    P.dve(lambda e, pj=pj, oc=oc: e.tensor_tensor(xtb[:, oc, :], pbank[pj][:, :], xtb[:, oc, :], ALU.add),
                          [pbuf[pj], B["xtb"]], [B["xtb"]])
                P.dma("B_xtb_st", XM[:, 1 + t0:1 + t0 + 512].rearrange("(k p) n -> p k n", p=128), xtb,
                      reads=[B["xtb"]], writes=[DB("XM", t)])
            P.barrier()

            ar.off = c_start
            WUP = ar.alloc([KC, 2 * DFF], BF16)
            WDN = ar.alloc([FC, D], BF16)
            WM = 510
            WM2 = WM + 2
            xm = ar.alloc([KC, WM2], F32)
            r16_off = ar.off
            xn2 = ar.alloc([KC, WM2], BF16)
            lnc = ar.alloc([WM2], F32)
            rsc = ar.alloc([WM2], F32)
            yc = [ar.alloc([WM2], F32) for _ in range(2)]
            assert ar.off - r16_off == 16384
            xres = ar.alloc([KC, WM2], F32, at=r16_off)
            sc_ = [ar.alloc([WM], BF16) for _ in range(2)]
            sqc = ar.alloc([KC, WM2], BF16)
            stg_off = ar.off
            h2 = ar.alloc([FC, WM2], BF16)
            stgc = [ar.alloc([2048], F32, at=stg_off), ar.alloc([2048], F32, at=stg_off + 8192)]
            assert stg_off + 16384 <= ar.off
            fl32 = ar.alloc([WM2], F32, at=stg_off)
            fr32 = ar.alloc([WM2], F32, at=stg_off + 2048)
            B = {n: Buf("C_" + n) for n in ("xm xn2 lnc rsc yc0 yc1 sc0 sc1 h2 sqc stg0 stg1").split()}
            R16B = [B["xn2"], B["lnc"], B["rsc"], B["yc0"], B["yc1"]]
            stgB = [B["stg0"], B["stg1"]]
            load_w(WUP, w_up[l * D:(l + 1) * D, :], KC, 2 * DFF, scale_col=SVO["g2"] + 8 * l, stg=stgc, stgB=stgB)
            load_w(WDN, w_dn[l * DFF:(l + 1) * DFF, :], FC, D, stg=stgc, stgB=stgB)
            P.barrier()
            pjr = [0]
            CR = [0, 1, 2, 4, 5, 6, 7]
            Xdst = Y if last_layer else Xs
            ctiles = []
            for u in range(2):
                c = 0
                while c < UNIT:
                    w = min(WM, UNIT - c)
                    ctiles.append((u * UNIT + c, w))
                    c += w

            def xm_reads(c0, W):
                ta, tb = max(c0 - 1, 0) // 512, min(c0 + W, NT - 1) // 512
                rd = [DB("XM", i) for i in range(ta, tb + 1)]
                if c0 == 0:
                    rd.append(DB("XM", -1))
                if c0 + W == NT:
                    rd.append(DB("XM", NTF))
                return rd

            def load_xm(c0, W):
                P.dma("C_xm", xm[:, :, 0:W + 2], XM[:, c0:c0 + W + 2].rearrange("(k p) n -> p k n", p=128),
                      reads=xm_reads(c0, W), writes=[B["xm"]])

            lnc2 = sqc.rearrange("p k n -> p (k n)")[:, 0:2 * WM2].bitcast(F32)[:, 0:WM2]
            rsc2 = sqc.rearrange("p k n -> p (k n)")[:, 2 * WM2:4 * WM2].bitcast(F32)[:, 0:WM2]

            def norm_part1(W2):
                P.act(lambda e, W2=W2: e.activation(sqc[:, :, 0:W2], xm[:, :, 0:W2], AF.Square), [B["xm"]], [B["sqc"]])
                for k in range(KC):
                    mm(pbank[3][:, 0:W2], ones, sqc[:, k, 0:W2], k == 0, k == KC - 1, [B["sqc"], B_const], [pbuf[3]])
                P.act(lambda e, W2=W2: e.activation(lnc2[:, 0:W2], pbank[3][:, 0:W2], AF.Ln, scale=1.0 / D, bias=EPS), [pbuf[3]], [B["sqc"]])
                P.act(lambda e, W2=W2: e.activation(rsc2[:, 0:W2], lnc2[:, 0:W2], AF.Exp, scale=-0.5), [B["sqc"]], [B["sqc"]])

            def norm_part2(ti_, W2):
                P.dve(lambda e, W2=W2: e.tensor_tensor(xn2[:, :, 0:W2], xm[:, :, 0:W2],
                                                        rsc2[:, 0:W2].unsqueeze(1).broadcast_to([128, KC, W2]), ALU.mult),
                      [B["xm"], B["sqc"]], [B["xn2"]])
                if ti_ + 1 < len(ctiles):
                    load_xm(*ctiles[ti_ + 1])

            hoist = not last_layer
            load_xm(*ctiles[0])
            for ti, (c0, W) in enumerate(ctiles):
                W2 = W + 2
                if ti == 0 or not hoist:
                    norm_part1(W2)
                    norm_part2(ti, W2)
                left_flag = (c0 == UNIT)
                right_flag = (c0 + W == UNIT)
                pend_h2 = None
                for f in range(FC):
                    ai = f % 2
                    pa = CR[pjr[0] % 7]
                    pjr[0] += 1
                    for k in range(KC):
                        mm(pbank[pa][:, 0:W2], WUP[:, k, f * 128:(f + 1) * 128], xn2[:, k, 0:W2], k == 0, k == KC - 1, [B["xn2"]], [pbuf[pa]])
                    pv = CR[pjr[0] % 7]
                    pjr[0] += 1
                    for k in range(KC):
                        mm(pbank[pv][:, 0:W], WUP[:, k, DFF + f * 128:DFF + (f + 1) * 128], xn2[:, k, 1:1 + W], k == 0, k == KC - 1,
                           [B["xn2"]], [pbuf[pv]])
                    cwo = SVO["cw"] + l * 3 * FC
                    w0 = sv[:, cwo + f:cwo + f + 1]
                    w1 = sv[:, cwo + FC + f:cwo + FC + f + 1]
                    w2 = sv[:, cwo + 2 * FC + f:cwo + 2 * FC + f + 1]
                    cbv = sv[:, SVO["cb"] + l * FC + f:SVO["cb"] + l * FC + f + 1]
                    if left_flag:
                        P.dve(lambda e, pa=pa: e.tensor_scalar(pbank[pa][:, 0:1], pbank[pa][:, 0:1], flag[:, 0:1], None, ALU.mult),
                              [pbuf[pa], B_const], [pbuf[pa]])
                    if right_flag:
                        P.dve(lambda e, pa=pa, W2=W2: e.tensor_scalar(pbank[pa][:, W2 - 1:W2], pbank[pa][:, W2 - 1:W2], flag[:, 0:1], None, ALU.mult),
                              [pbuf[pa], B_const], [pbuf[pa]])
                    P.dve(lambda e, pa=pa, ai=ai, W=W, w0=w0, cbv=cbv: e.tensor_scalar(yc[ai][:, 0:W], pbank[pa][:, 0:W], w0, cbv, ALU.mult, ALU.add),
                          [pbuf[pa], B_const], [B["yc%d" % ai]])
                    P.dve(lambda e, pa=pa, ai=ai, W=W, w1=w1: e.scalar_tensor_tensor(yc[ai][:, 0:W], pbank[pa][:, 1:1 + W], w1, yc[ai][:, 0:W], ALU.mult, ALU.add),
                          [pbuf[pa], B["yc%d" % ai], B_const], [B["yc%d" % ai]])
                    P.dve(lambda e, pa=pa, ai=ai, W=W, w2=w2: e.scalar_tensor_tensor(yc[ai][:, 0:W], pbank[pa][:, 2:2 + W], w2, yc[ai][:, 0:W], ALU.mult, ALU.add),
                          [pbuf[pa], B["yc%d" % ai], B_const], [B["yc%d" % ai]])
                    P.act(lambda e, ai=ai, W=W: e.activation(sc_[ai][:, 0:W], yc[ai][:, 0:W], AF.Silu), [B["yc%d" % ai]], [B["sc%d" % ai]])
                    if pend_h2 is not None:
                        pend_h2()
                    pend_h2 = (lambda ai=ai, pv=pv, f=f, W=W: P.dve(
                        lambda e: e.tensor_tensor(h2[:, f, 0:W], pbank[pv][:, 0:W], sc_[ai][:, 0:W], ALU.mult),
                        [pbuf[pv], B["sc%d" % ai]], [B["h2"]]))
                pend_h2()
                P.dma("C_xres", xres[:, :, 0:W], XM[:, c0 + 1:c0 + 1 + W].rearrange("(k p) n -> p k n", p=128),
                      reads=xm_reads(c0, W), writes=R16B)
                for oc in range(KC):
                    if oc == 2 and hoist and ti + 1 < len(ctiles):
                        norm_part1(ctiles[ti + 1][1] + 2)
                    pj = CR[pjr[0] % 7]
                    pjr[0] += 1
                    for f in range(FC):
                        mm(pbank[pj][:, 0:W], WDN[:, f, oc * 128:(oc + 1) * 128], h2[:, f, 0:W], f == 0, f == FC - 1, [B["h2"]], [pbuf[pj]])
                    rb_ = R16B[min(max(oc - 3, 0), 4)] if not last_layer else None
                    P.dve(lambda e, pj=pj, oc=oc, W=W: e.tensor_tensor(xres[:, oc, 0:W], pbank[pj][:, 0:W], xres[:, oc, 0:W], ALU.add),
                          [pbuf[pj]] + (R16B if last_layer else [rb_]), R16B if last_layer else [rb_])
                    if not last_layer:
                        P.dma("C_xo%d" % oc, Xdst[oc * 128:(oc + 1) * 128, c0:c0 + W], xres[:, oc, 0:W], reads=[rb_],
                              writes=[DB("X", i) for i in range(c0 // 512, (c0 + W - 1) // 512 + 1)])
                        if oc == 3 and ti + 1 < len(ctiles):
                            norm_part2(ti + 1, ctiles[ti + 1][1] + 2)
                if last_layer:
                    P.act(lambda e, W=W: e.activation(sqc[:, :, 0:W], xres[:, :, 0:W], AF.Square), R16B, [B["sqc"]])
                    for k in range(KC):
                        mm(pbank[3][:, 0:W], ones, sqc[:, k, 0:W], k == 0, k == KC - 1, [B["sqc"], B_const], [pbuf[3]])
                    P.act(lambda e, W=W: e.activation(fl32[:, 0:W], pbank[3][:, 0:W], AF.Ln, scale=1.0 / D, bias=EPS),
                          [pbuf[3]], [B["h2"]])
                    P.act(lambda e, W=W: e.activation(fr32[:, 0:W], fl32[:, 0:W], AF.Exp, scale=-0.5), [B["h2"]], [B["h2"]])
                    for k in range(KC):
                        fgk = sv[:, SVO["fg"] + k:SVO["fg"] + k + 1]
                        P.dve(lambda e, k=k, fgk=fgk, W=W: e.scalar_tensor_tensor(xres[:, k, 0:W], xres[:, k, 0:W], fgk, fr32[:, 0:W], ALU.mult, ALU.mult),
                              R16B + [B["h2"], B_const], R16B)
                if last_layer:
                    wr = [DB("Y", i) for i in range(c0 // 512, (c0 + W - 1) // 512 + 1)]
                    P.dma("C_xo", Xdst[:, c0:c0 + W].rearrange("(k p) n -> p k n", p=128), xres[:, :, 0:W], reads=R16B, writes=wr)
            P.barrier()
        P.add("sp", None, reads=[DB("Y", t) for t in range(NTF)])
        stats = P.emit(st)
    return nc, stats


def host_inputs(inp, cfg, x_units, flags):
    L = cfg["DEPTH"]
    UNIT = cfg["UNIT"]
    svec = small_layout(inp, L)
    eye = np.eye(128, dtype=np.float32)
    jj, ii = np.meshgrid(np.arange(128), np.arange(128), indexing="ij")
    cst = np.concatenate([eye, np.ones((128, 128), np.float32), (jj <= ii).astype(np.float32), (jj >= ii).astype(np.float32)], axis=1)
    bE, bF, bL = band_matrices("E"), band_matrices("first"), band_matrices("last")
    wlr = np.zeros((L, 32, 2, 512), np.float32)
    wlr[:, 0:16, 0, :] = inp["w_lr2_f"]
    wlr[:, 16:32, 1, :] = inp["w_lr2_b"]
    shared = dict(
        svec=svec, cst=np.ascontiguousarray(cst),
        w_in=np.ascontiguousarray(inp["w_in"]).reshape(L * D, INW),
        wlr=wlr.reshape(L * 32, 1024),
        w_grp=np.ascontiguousarray(inp["w_pool_grp"]).reshape(L * 512, 128),
        w_ba=np.ascontiguousarray(inp["w_br_a"]).reshape(L * D, D),
        w_bb=np.ascontiguousarray(inp["w_br_b"]).reshape(L * 512, D),
        w_out=np.ascontiguousarray(inp["w_out"]).reshape(L * D, D),
        w_up=np.ascontiguousarray(inp["w_up"]).reshape(L * D, 2 * DFF),
        w_dn=np.ascontiguousarray(inp["w_down"]).reshape(L * DFF, D),
    )
    maps = []
    for units, fl in zip(x_units, flags):
        xs = [u if u is not None else np.zeros((UNIT, D), np.float32) for u in units]
        xT = np.ascontiguousarray(np.concatenate(xs, axis=0).T)
        bands = np.stack([bF, bE if fl else bL, bE if fl else bF, bL, bE], axis=0)
        m = dict(shared)
        m["xT"] = xT
        m["bands"] = np.ascontiguousarray(bands.reshape(5 * 4 * 144, 128))
        m["flag"] = np.full((128, 1), 1.0 if fl else 0.0, np.float32)
        maps.append(m)
    return maps


_PROG_CACHE = {}


def kernel(**inputs):
    cfg = dict(CFG)
    inp = {k: np.asarray(v, dtype=np.float32) for k, v in inputs.items()}
    UNIT = cfg["UNIT"]
    xp = inp["x_prompt"][0]
    xsmp = inp["x_sample"]
    x_units = [[xp[0:UNIT], xp[UNIT:2 * UNIT]], [xsmp[0], xsmp[1]]] + [[xsmp[i], None] for i in range(2, 8)]
    flags = [True] + [False] * 7
    maps = host_inputs(inp, cfg, x_units, flags)
    key = (cfg["UNIT"], cfg["DEPTH"])
    if key not in _PROG_CACHE:
        _PROG_CACHE[key] = build_program(cfg)[0]
    nc = _PROG_CACHE[key]
    res = run_bass_kernel_spmd(nc, maps, core_ids=list(range(8)))
    ys = [np.asarray(r["Y"]) for r in res.results]
    y_prompt = np.ascontiguousarray(ys[0].T)[None]
    y_sample = np.empty_like(xsmp)
    y_sample[0] = ys[1][:, 0:UNIT].T
    y_sample[1] = ys[1][:, UNIT:2 * UNIT].T
    for i in range(2, 8):
        y_sample[i] = ys[i][:, 0:UNIT].T
    return (y_prompt.astype(np.float32), y_sample.astype(np.float32))
```
